# Optimizing a Trainium2 kernel written in Bass

```python
import jax, jax.numpy as jnp
from jax import lax
import numpy as np

D_MODEL = 1024
BATCH = 16
SEQ = 2048
DEPTH = 1

CTX_LEN = 256
GRID_W = 64
ROWS_PER_CHUNK = 2
GMLP_CHUNK = ROWS_PER_CHUNK * GRID_W
MLSTM_CHUNK = 128
MLSTM_WIDTH = D_MODEL
N_HEADS = 4
HEAD_DIM = MLSTM_WIDTH // N_HEADS
N_GATES = 4 * N_HEADS
CONV_W = 3
GMLP_WIDTH = D_MODEL
GMLP_GROUP_DIM = 128
GMLP_GROUPS = GMLP_WIDTH // GMLP_GROUP_DIM
D_FF = 4 * D_MODEL
NEG = -1e30
EPS = 1e-6

OFF_Q = 0
OFF_K = OFF_Q + MLSTM_WIDTH
OFF_V = OFF_K + MLSTM_WIDTH
OFF_G = OFF_V + MLSTM_WIDTH
OFF_O = OFF_G + N_GATES
OFF_U = OFF_O + MLSTM_WIDTH
OFF_VG = OFF_U + GMLP_WIDTH
OFF_GA = OFF_VG + GMLP_WIDTH
OFF_GB = OFF_GA + D_MODEL
P_TOTAL = OFF_GB + D_MODEL

kernel_name = 'hybrid_mlstm_gmlp_dit_block'


def rmsnorm(x, g):
    xf = x.astype(jnp.float32)
    y = xf * lax.rsqrt(jnp.mean(xf * xf, axis=-1, keepdims=True) + EPS)
    return (y * g.astype(jnp.float32)).astype(x.dtype)


def modulate(xn, shift, scale):
    return xn * (1.0 + scale) + shift


def short_conv(x, w):
    pad = CONV_W // 2
    L = x.shape[1]
    xp = jnp.pad(x, ((0, 0), (pad, pad), (0, 0)))
    out = xp[:, 0:L] * w[0]
    for j in range(1, CONV_W):
        out = out + xp[:, j:j + L] * w[j]
    return out


def mlstm_inputs(p, conv_w, b_gate):
    B, L, _ = p.shape
    qk = jax.nn.silu(short_conv(p[..., OFF_Q:OFF_V], conv_w))
    q = qk[..., :MLSTM_WIDTH].reshape(B, L, N_HEADS, HEAD_DIM)
    k = (qk[..., MLSTM_WIDTH:] * (HEAD_DIM ** -0.5)).reshape(B, L, N_HEADS, HEAD_DIM)
    v = p[..., OFF_V:OFF_G].reshape(B, L, N_HEADS, HEAD_DIM)
    g = p[..., OFF_G:OFF_O].astype(jnp.float32) + b_gate.astype(jnp.float32)
    gi_f, gf_f, gi_b, gf_b = jnp.split(g, 4, axis=-1)
    fwd = (gi_f, jax.nn.log_sigmoid(gf_f))
    bwd = (gi_b, jax.nn.log_sigmoid(gf_b))
    return q, k, v, fwd, bwd


def init_state(B):
    C = jnp.zeros((B, N_HEADS, HEAD_DIM, HEAD_DIM), jnp.float32)
    n = jnp.zeros((B, N_HEADS, HEAD_DIM), jnp.float32)
    m = jnp.zeros((B, N_HEADS), jnp.float32)
    return (C, n, m)


def mlstm_chunked(q, k, v, log_i, log_f, state, return_h):
    B, L, H, d = q.shape
    nc = L // MLSTM_CHUNK

    def to_chunks(a):
        a = a.astype(jnp.float32).reshape((B, nc, MLSTM_CHUNK) + a.shape[2:])
        return jnp.moveaxis(a, (1, 3), (0, 2))

    xs = (to_chunks(q), to_chunks(k), to_chunks(v), to_chunks(log_i), to_chunks(log_f))
    tril = jnp.tril(jnp.ones((MLSTM_CHUNK, MLSTM_CHUNK), dtype=bool))

    def body(carry, inp):
        C, n, m = carry
        qc, kc, vc, li, lf = inp
        b = jnp.cumsum(lf, axis=-1)
        b_last = b[..., -1]
        w_end = b_last[..., None] - b + li
        m_new = jnp.maximum(b_last + m, jnp.max(w_end, axis=-1))
        e = jnp.exp(w_end - m_new[..., None])
        s_prev = jnp.exp(b_last + m - m_new)
        C_new = s_prev[..., None, None] * C + jnp.einsum('bhj,bhjd,bhjv->bhdv', e, kc, vc)
        n_new = s_prev[..., None] * n + jnp.einsum('bhj,bhjd->bhd', e, kc)
        if not return_h:
            return (C_new, n_new, m_new), None
        d_intra = jnp.where(tril, b[..., :, None] - b[..., None, :] + li[..., None, :], NEG)
        inter = b + m[..., None]
        m_i = jnp.maximum(inter, jnp.max(d_intra, axis=-1))
        w = jnp.exp(d_intra - m_i[..., None])
        s_inter = jnp.exp(inter - m_i)
        a = w * jnp.einsum('bhid,bhjd->bhij', qc, kc)
        num = jnp.einsum('bhij,bhjv->bhiv', a, vc) + s_inter[..., None] * jnp.einsum('bhid,bhdv->bhiv', qc, C)
        den = jnp.sum(a, axis=-1) + s_inter * jnp.einsum('bhid,bhd->bhi', qc, n)
        h = num / jnp.maximum(jnp.abs(den), jnp.exp(-m_i))[..., None]
        return (C_new, n_new, m_new), h

    state, hs = lax.scan(body, state, xs)
    if not return_h:
        return None, state
    h = jnp.moveaxis(hs, (0, 2), (1, 3)).reshape(B, L, H, d)
    return h, state


def bidir_mlstm(lat, ctx_in, need_ctx_h):
    qx, kx, vx, (lixf, lfxf), (lixb, lfxb) = lat
    qc, kc, vc, (licf, lfcf), (licb, lfcb) = ctx_in
    st0 = init_state(qx.shape[0])
    rev = lambda a: a[:, ::-1]
    hcf, st_f = mlstm_chunked(qc, kc, vc, licf, lfcf, st0, need_ctx_h)
    hxf, _ = mlstm_chunked(qx, kx, vx, lixf, lfxf, st_f, True)
    hcb, st_b = mlstm_chunked(rev(qc), rev(kc), rev(vc), rev(licb), rev(lfcb), st0, need_ctx_h)
    hxb, _ = mlstm_chunked(rev(qx), rev(kx), rev(vx), rev(lixb), rev(lfxb), st_b, True)
    hx = hxf + rev(hxb)
    hc = (hcf + rev(hcb)) if need_ctx_h else None
    return hx, hc


def spatial_gate(u, vg, g_sgu, w_s, b_s, n_chunks):
    B, L, _ = u.shape
    vn = rmsnorm(vg, g_sgu).reshape(B, n_chunks, GMLP_CHUNK, GMLP_GROUPS, GMLP_GROUP_DIM)
    s = jnp.einsum('gpq,bnqgc->bnpgc', w_s, vn) + jnp.transpose(b_s)[None, None, :, :, None]
    return u * s.reshape(B, L, GMLP_WIDTH)


def merge_branches(p, h_m, n_chunks, g_mh, w_a, w_s, b_s, g_sgu, w_b, w_out):
    B, L, _ = p.shape
    o = p[..., OFF_O:OFF_U]
    u = jax.nn.gelu(p[..., OFF_U:OFF_VG])
    vg = jax.nn.gelu(p[..., OFF_VG:OFF_GA])
    ga = p[..., OFF_GA:OFF_GB]
    gb = p[..., OFF_GB:P_TOTAL]
    hm = (jax.nn.sigmoid(o) * h_m.reshape(B, L, MLSTM_WIDTH).astype(p.dtype)).reshape(B, L, N_HEADS, HEAD_DIM)
    hm = rmsnorm(hm, g_mh.reshape(N_HEADS, HEAD_DIM)).reshape(B, L, MLSTM_WIDTH)
    ya = hm @ w_a
    yb = spatial_gate(u, vg, g_sgu, w_s, b_s, n_chunks) @ w_b
    y = jax.nn.sigmoid(ga) * ya + jax.nn.sigmoid(gb) * yb
    return y @ w_out


def sq_relu_mlp(xn, w1, w2):
    h = jax.nn.relu(xn @ w1)
    return (h * h) @ w2


def setup_inputs(seed: int = 0) -> dict:
    key = jax.random.key(seed)
    ks = jax.random.split(key, 24)
    f32 = jnp.float32
    nrm = lambda k, shape, scale: jax.random.normal(k, shape, f32) * scale
    i_bias = nrm(ks[10], (DEPTH, N_HEADS), 0.1)
    f_bias = 3.0 + 3.0 * jnp.linspace(0.0, 1.0, N_HEADS, dtype=f32)[None] + nrm(ks[11], (DEPTH, N_HEADS), 0.1)
    i_bias_b = nrm(ks[12], (DEPTH, N_HEADS), 0.1)
    f_bias_b = 3.0 + 3.0 * jnp.linspace(0.0, 1.0, N_HEADS, dtype=f32)[None] + nrm(ks[13], (DEPTH, N_HEADS), 0.1)
    return {
        'x': nrm(ks[0], (BATCH, SEQ, D_MODEL), 1.0),
        'c': nrm(ks[1], (BATCH, D_MODEL), 1.0),
        'ctx': nrm(ks[2], (BATCH, CTX_LEN, D_MODEL), 1.0),
        'c_ctx': nrm(ks[3], (D_MODEL,), 1.0),
        'norm1': 1.0 + nrm(ks[4], (DEPTH, D_MODEL), 0.05),
        'norm2': 1.0 + nrm(ks[5], (DEPTH, D_MODEL), 0.05),
        'w_mod': nrm(ks[6], (DEPTH, D_MODEL, 6 * D_MODEL), 0.5 * D_MODEL ** -0.5),
        'b_mod': nrm(ks[7], (DEPTH, 6 * D_MODEL), 0.02),
        'w_in': nrm(ks[8], (DEPTH, D_MODEL, P_TOTAL), D_MODEL ** -0.5),
        'conv_qk': nrm(ks[9], (DEPTH, CONV_W, 2 * MLSTM_WIDTH), CONV_W ** -0.5),
        'b_gate': jnp.concatenate([i_bias, f_bias, i_bias_b, f_bias_b], axis=-1),
        'g_mh': 1.0 + nrm(ks[14], (DEPTH, MLSTM_WIDTH), 0.05),
        'w_a': nrm(ks[15], (DEPTH, MLSTM_WIDTH, D_MODEL), MLSTM_WIDTH ** -0.5),
        'w_s': nrm(ks[16], (DEPTH, GMLP_GROUPS, GMLP_CHUNK, GMLP_CHUNK), GMLP_CHUNK ** -0.5),
        'b_s': 1.0 + nrm(ks[17], (DEPTH, GMLP_GROUPS, GMLP_CHUNK), 0.05),
        'g_sgu': 1.0 + nrm(ks[18], (DEPTH, GMLP_WIDTH), 0.05),
        'w_b': nrm(ks[19], (DEPTH, GMLP_WIDTH, D_MODEL), GMLP_WIDTH ** -0.5),
        'w_out': nrm(ks[20], (DEPTH, D_MODEL, D_MODEL), D_MODEL ** -0.5),
        'w1': nrm(ks[21], (DEPTH, D_MODEL, D_FF), D_MODEL ** -0.5),
        'w2': nrm(ks[22], (DEPTH, D_FF, D_MODEL), D_FF ** -0.5),
        'norm_f': 1.0 + nrm(ks[23], (D_MODEL,), 0.05),
    }


def reference(x, c, ctx, c_ctx, norm1, norm2, w_mod, b_mod, w_in, conv_qk, b_gate, g_mh,
              w_a, w_s, b_s, g_sgu, w_b, w_out, w1, w2, norm_f):
    B, S, D = x.shape
    rows = S // GRID_W
    n_lat_chunks = rows // ROWS_PER_CHUNK
    n_ctx_chunks = ctx.shape[1] // GMLP_CHUNK
    s_c = jax.nn.silu(c)
    s_cc = jax.nn.silu(c_ctx)
    for l in range(DEPTH):
        last = l == DEPTH - 1
        mod_x = s_c @ w_mod[l] + b_mod[l]
        sh1, sc1, g1, sh2, sc2, g2 = jnp.split(mod_x[:, None, :], 6, axis=-1)
        n_mod_c = 2 if last else 6
        mod_c = s_cc @ w_mod[l][:, :n_mod_c * D] + b_mod[l][:n_mod_c * D]
        mods_c = jnp.split(mod_c, n_mod_c, axis=-1)
        sh1c, sc1c = mods_c[0], mods_c[1]

        xn = modulate(rmsnorm(x, norm1[l]), sh1, sc1)
        cn = modulate(rmsnorm(ctx, norm1[l]), sh1c, sc1c)
        px = xn @ w_in[l]
        pc = cn @ (w_in[l][:, :OFF_O] if last else w_in[l])
        lat_in = mlstm_inputs(px, conv_qk[l], b_gate[l])
        ctx_in = mlstm_inputs(pc, conv_qk[l], b_gate[l])
        hx, hc = bidir_mlstm(lat_in, ctx_in, not last)
        mix_x = merge_branches(px, hx, n_lat_chunks, g_mh[l], w_a[l], w_s[l], b_s[l],
                               g_sgu[l], w_b[l], w_out[l])
        x = x + g1 * mix_x

        xn2 = modulate(rmsnorm(x, norm2[l]), sh2, sc2)
        x = x + g2 * sq_relu_mlp(xn2, w1[l], w2[l])

        if not last:
            g1c, sh2c, sc2c, g2c = mods_c[2], mods_c[3], mods_c[4], mods_c[5]
            mix_c = merge_branches(pc, hc, n_ctx_chunks, g_mh[l], w_a[l], w_s[l], b_s[l],
                                   g_sgu[l], w_b[l], w_out[l])
            ctx = ctx + g1c * mix_c
            cn2 = modulate(rmsnorm(ctx, norm2[l]), sh2c, sc2c)
            ctx = ctx + g2c * sq_relu_mlp(cn2, w1[l], w2[l])
    return rmsnorm(x, norm_f)
```

```python
import numpy as np
import ml_dtypes
from contextlib import ExitStack
import concourse.bass as bass
import concourse.mybir as mybir
from concourse.bass_utils import run_bass_kernel_spmd

F32 = mybir.dt.float32
BF16 = mybir.dt.bfloat16
AF = mybir.ActivationFunctionType
ALU = mybir.AluOpType

D = 1024
S = 2048
CL = 256
NB = 2
SEQ = S + CL
NCH = SEQ // 128
EPS = 1e-6
DEBUG = False


class Op:
    __slots__ = ("eng", "fn", "deps", "group", "dma", "needs_inc", "sem", "val",
                 "prev_same_sem")

    def __init__(self, eng, fn, group, dma):
        self.eng = eng
        self.fn = fn
        self.deps = []
        self.group = group
        self.dma = dma
        self.needs_inc = False
        self.sem = None
        self.val = 0
        self.prev_same_sem = None


class Prog:
    ENGS = ("pe", "act", "dve", "pool", "sp")

    def __init__(self, nc, n_dma_sems=14):
        self.nc = nc
        self.ops = {e: [] for e in self.ENGS}
        self.last_w = {}
        self.readers = {}
        self.n_dma_sems = n_dma_sems
        self.pending_dmas = []
        self.bar = None
        self.bar_seen = {e: None for e in self.ENGS}

    def add(self, eng, fn, reads=(), writes=(), group=None, dma=False):
        op = Op(eng, fn, group, dma)
        deps = {}
        for k in reads:
            w = self.last_w.get(k)
            if w is not None:
                deps[id(w)] = w
        for k in writes:
            w = self.last_w.get(k)
            if w is not None and not (group is not None and w.group == group):
                deps[id(w)] = w
            rd = self.readers.get(k)
            if rd:
                for r in rd[0].values():
                    deps[id(r)] = r
                for r in rd[1]:
                    deps[id(r)] = r
        if self.bar is not None and self.bar_seen[eng] is not self.bar:
            deps[id(self.bar)] = self.bar
            self.bar_seen[eng] = self.bar
        for d in deps.values():
            if d is op:
                continue
            if (not d.dma) and (not dma) and d.eng == "pe" and eng == "pe":
                continue
            op.deps.append(d)
        for k in reads:
            rd = self.readers.get(k)
            if rd is None:
                rd = self.readers[k] = ({}, [])
            if dma:
                rd[1].append(op)
            else:
                rd[0][eng] = op
        for k in writes:
            self.last_w[k] = op
            self.readers[k] = None
        self.ops[eng].append(op)
        if dma:
            self.pending_dmas.append(op)
        return op

    def dma(self, out, in_, reads=(), writes=(), eng="sp"):
        return self.add(eng, OPC("dma_start", out=out, in_=in_), reads, writes, dma=True)

    def barrier(self, dummy):
        j = Op("dve", OPC("memset", dummy, 0.0), None, False)
        for e in self.ENGS:
            if e == "sp":
                continue
            for o in reversed(self.ops[e]):
                if not o.dma:
                    j.deps.append(o)
                    break
        j.deps.extend(self.pending_dmas)
        self.pending_dmas = []
        self.ops["dve"].append(j)
        self.bar = j
        self.bar_seen = {e: None for e in self.ENGS}
        self.last_w = {}
        self.readers = {}

    def emit(self):
        nc = self.nc
        for e in self.ENGS:
            for op in self.ops[e]:
                for d in op.deps:
                    d.needs_inc = True
        with ExitStack() as st:
            esem = {e: st.enter_context(nc.semaphore("s_" + e)) for e in self.ENGS}
            dsem = {e: [st.enter_context(nc.semaphore("d_%s_%d" % (e, i)))
                        for i in range(self.n_dma_sems)] for e in ("sp", "act", "pool")}
            for e in self.ENGS:
                cnt = 0
                ndma = 0
                dcnt = [0] * self.n_dma_sems
                lastd = [None] * self.n_dma_sems
                for op in self.ops[e]:
                    if op.dma:
                        i = ndma % self.n_dma_sems
                        ndma += 1
                        dcnt[i] += 16
                        op.sem = dsem[e][i]
                        op.val = dcnt[i]
                        op.prev_same_sem = lastd[i]
                        lastd[i] = op
                        op.needs_inc = True
                    elif op.needs_inc:
                        cnt += 1
                        op.sem = esem[e]
                        op.val = cnt
            block = st.enter_context(nc.Block())

            def run(ename, eng):
                seen = {}
                for op in self.ops[ename]:
                    waits = {}
                    ds = list(op.deps)
                    if op.dma and op.prev_same_sem is not None:
                        ds.append(op.prev_same_sem)
                    for d in ds:
                        key = id(d.sem)
                        if seen.get(key, 0) >= d.val:
                            continue
                        if key not in waits or waits[key][1] < d.val:
                            waits[key] = (d.sem, d.val)
                    for key, (s, v) in waits.items():
                        eng.wait_ge(s, v)
                        seen[key] = v
                    ins = op.fn(eng)
                    if op.needs_inc:
                        ins.then_inc(op.sem, 16 if op.dma else 1)
                lastd = {}
                for op in self.ops[ename]:
                    if op.dma:
                        lastd[id(op.sem)] = (op.sem, op.val)
                for s, v in lastd.values():
                    eng.wait_ge(s, v)

            @block.sync
            def _(eng):
                run("sp", eng)

            @block.scalar
            def _(eng):
                run("act", eng)

            @block.vector
            def _(eng):
                run("dve", eng)

            @block.gpsimd
            def _(eng):
                run("pool", eng)

            @block.tensor
            def _(eng):
                run("pe", eng)


def OPC(name, *args, **kwargs):
    return lambda e: getattr(e, name)(*args, **kwargs)


def bc(t, off, dims):
    ps = 1
    for s_ in t.shape[1:]:
        ps *= s_
    return bass.AP(t, off, [[ps, 128]] + [list(d) for d in dims])


WIN_COLS = dict(q=0, k=1024, v=2048, o=3088, u=4112, vg=5136, ga=6160, gb=7184)
WB = {}
_i = 0
for _n in ("q", "k", "v", "o", "u", "vg", "ga", "gb"):
    WB[_n] = _i
    _i += 2
WB["wa"] = 16
WB["wb"] = 18
WB["wout"] = 20
WB["w1"] = 22
WB["w2"] = 30
NWB = 38

C_N1, C_N2, C_GMH, C_CONV = 0, 8, 16, 24
NCOLS = 24 + 48
R_BG, R_GSGU, R_NF, R_BS = 0, 16, 16 + 1024, 16 + 2048
NROWS = 16 + 3072


def seq_tiles(L, T=384):
    out = []
    t = 0
    while t < L:
        n = min(T, L - t)
        out.append((t, n))
        t += n
    return out


def build_nc():
    nc = bass.Bass("TRN2", target_bir_lowering=False)
    dk = "ExternalOutput" if DEBUG else "Internal"

    def din(name, shape, dt=F32):
        return nc.dram_tensor(name, list(shape), dt, kind="ExternalInput").ap()

    def dscr(name, shape, dt):
        return nc.dram_tensor(name, list(shape), dt, kind=dk).ap()

    x_d = din("x", [NB, S, D])
    ctx_d = din("ctx", [NB, CL, D])
    scT_d = din("scT", [128, 8, 3])
    wmod_d = din("wmod", [12, 128, 8, 512])
    bmodc_d = din("bmodc", [128, 48])
    bmodr_d = din("bmodr", [1, 6 * D])
    wall_d = din("wall", [NWB, 128, 8, 512])
    wg_d = din("wg", [128, 8, 16])
    cols_d = din("cols", [128, NCOLS])
    rows_d = din("rows", [1, NROWS])
    wsT_d = din("wsT", [128, 8, 128])
    consts_d = din("consts", [128, 4, 128])
    out_d = nc.dram_tensor("out", [NB, S, D], F32, kind="ExternalOutput").ap()

    KT_d = dscr("KT", [NB * NCH, 128, 8, 128], BF16)
    QT_d = dscr("QT", [NB * NCH, 128, 8, 128], BF16)
    VP_d = dscr("VP", [NB * NCH, 128, 2 * 4 * 257], BF16)
    SO_d = dscr("SO", [NB * 16, 128, D], BF16)
    UT_d = dscr("UT", [NB * 4, 128, 8, 512], BF16)
    VN_d = dscr("VN", [NB * 16, 128, D], BF16)
    SGA_d = dscr("SGA", [NB * 4, 128, 8, 512], BF16)
    SGB_d = dscr("SGB", [NB * 4, 128, 8, 512], BF16)
    CB_d = dscr("CB", [NB * 16, 128, 4 * 2 * 257], BF16)
    HMT_d = dscr("HMT", [NB * 16, 128, 8, 128], BF16)
    X1_d = dscr("X1", [NB * 16, 128, D], F32)
    XN2_d = dscr("XN2", [NB * 8, 128, 8, 256], BF16)
    H1T_d = dscr("H1T", [NB * 8, 128, 32, 256], BF16)

    P = Prog(nc)
    es = ExitStack()
    alloc = {"top": 16512, "n": 0}
    stack = []

    class phase:
        def __enter__(self):
            stack.append(alloc["top"])
            return self

        def __exit__(self, *a):
            alloc["top"] = stack.pop()
            return False

    with es:
        def sb(name, shape, dt, stk=None):
            nb_ = 2 if dt == BF16 else 4
            for s_ in shape[1:]:
                nb_ *= s_
            nb_ = (nb_ + 63) // 64 * 64
            off = alloc["top"]
            alloc["top"] += nb_
            assert alloc["top"] <= 229344, ("SBUF overflow", name, alloc["top"])
            alloc["n"] += 1
            return nc.alloc_sbuf_tensor_at("%s_%d" % (name, alloc["n"]), list(shape), dt, offset=off)

        psb = [es.enter_context(nc.psum_tensor("ps%d" % i, [128, 512], F32)) for i in range(8)]
        pk = ["ps%d" % i for i in range(8)]

        consts = sb("consts", [128, 4, 128], F32)
        identb = sb("identb", [128, 128], BF16)
        mF = sb("mF", [128, 128], F32)
        mB = sb("mB", [128, 128], F32)
        cols = sb("cols", [128, NCOLS], F32)
        bmodc = sb("bmodc", [128, 48], F32)
        modc = sb("modc", [128, 48, 3], F32)
        gcol1 = sb("gcol1", [128, 8, 3], F32)
        gcol2 = sb("gcol2", [128, 8, 3], F32)
        g1bc = sb("g1bc", [128, NB, D], F32)
        g2bc = sb("g2bc", [128, NB, D], F32)
        rowsb = sb("rowsb", [128, NROWS], F32)
        wsT = sb("wsTb", [128, 8, 128], BF16)
        wgb = sb("wgb", [128, 8, 16], BF16)
        GATES = sb("GATES", [128, NB * NCH, 24], F32)
        dummy = sb("dummy", [128, 1], F32)
        GS = NB * NCH * 24

        ident = consts[:, 0, :]
        triU = consts[:, 1, :]
        triL = consts[:, 2, :]
        ones = consts[:, 3, :]

        P.dma(consts[:], consts_d, writes=["consts"])
        P.dma(cols[:], cols_d, writes=["cols"])
        P.dma(bmodc[:], bmodc_d, writes=["bmodc"])
        P.dma(rowsb[:], bass.AP(rows_d.tensor, 0, [[0, 128], [1, NROWS]]), writes=["rowsb"])
        P.add("dve", OPC("tensor_copy", out=identb[:], in_=ident), reads=["consts"], writes=["identb"])
        P.add("dve", OPC("tensor_scalar", out=mF[:], in0=triU, scalar1=0.0625, scalar2=None, op0=ALU.mult),
              reads=["consts"], writes=["mF"])
        P.add("dve", OPC("tensor_scalar", out=mB[:], in0=triL, scalar1=0.0625, scalar2=None, op0=ALU.mult),
              reads=["consts"], writes=["mB"])

        with phase() as ph:
            scT = sb("scT", [128, 8, 3], F32, ph)
            screp = sb("screp", [128, NB, 8, 128], F32, ph)
            wst = [sb("wst%d" % i, [128, 8, 512], F32, ph) for i in range(2)]
            wstmp = sb("wstmp", [128, 8, 128], F32, ph)
            wgst = sb("wgst", [128, 8, 16], F32, ph)
            bmr = sb("bmr", [128, 2 * D], F32, ph)
            modrow = sb("modrow", [3, 4 * D], F32, ph)
            bmr3 = sb("bmr3", [3, 4 * D], F32, ph)
            for mm, m in enumerate((0, 1, 3, 4)):
                P.dma(bmr3[0:3, mm * D:(mm + 1) * D], bass.AP(bmodr_d.tensor, m * D, [[0, 3], [1, D]]), writes=[("bmr3", mm)])
            P.dma(bmr[:, 0:D], bass.AP(bmodr_d.tensor, 2 * D, [[0, 128], [1, D]]), writes=["bmr"])
            P.dma(bmr[:, D:2 * D], bass.AP(bmodr_d.tensor, 5 * D, [[0, 128], [1, D]]), writes=["bmr2"])
            P.dma(scT[:], scT_d, writes=["scT"])
            P.dma(wstmp[:], wsT_d, writes=["wstmp"])
            P.dma(wgst[:], wg_d, writes=["wgst"])
            P.add("pool", OPC("tensor_copy", out=wsT[:], in_=wstmp[:]), reads=["wstmp"], writes=["wsT"])
            P.add("pool", OPC("tensor_copy", out=wgb[:], in_=wgst[:]), reads=["wgst"], writes=["wgb"])
            P.add("act", OPC("activation", out=scT[:], in_=scT[:], func=AF.Silu), reads=["scT"], writes=["scT"])
            for b in range(NB):
                P.add("dve", OPC("tensor_copy", out=screp[:, b, :, :], in_=bc(scT, b, [[3, 8], [0, 128]])), reads=["scT"], writes=["screp"])
            for j in range(12):
                m = j // 2
                w = wst[j % 2]
                wk = "wst%d" % (j % 2)
                P.dma(w[:], wmod_d[j], writes=[wk])
                if m in (2, 5):
                    dst = g1bc if m == 2 else g2bc
                    for b in range(NB):
                        bank = (2 * j + b) % 4
                        for kc in range(8):
                            P.add("pe", OPC("matmul", psb[bank][:, :], lhsT=screp[:, b, kc, :], rhs=w[:, kc, :],
                                start=(kc == 0), stop=(kc == 7)),
                                reads=[wk, "screp"], writes=[pk[bank]], group=("m0", j, b))
                        c0 = (j % 2) * 512
                        r0 = (0 if m == 2 else D) + c0
                        P.add("dve", OPC("tensor_tensor", out=dst[:, b, c0:c0 + 512], in0=psb[bank][:, :],
                            in1=bmr[:, r0:r0 + 512], op=ALU.add),
                            reads=["bmr", "bmr2"], writes=[pk[bank], "gbc"])
                else:
                    bank = 4 + (j % 2)
                    mm = {0: 0, 1: 1, 3: 2, 4: 3}[m]
                    for kc in range(8):
                        P.add("pe", OPC("matmul", psb[bank][0:3, :], lhsT=scT[:, kc, :], rhs=w[:, kc, :],
                                        start=(kc == 0), stop=(kc == 7)),
                              reads=[wk, "scT"], writes=[pk[bank]], group=("m0r", j))
                    c0 = mm * D + (j % 2) * 512
                    P.add("dve", OPC("tensor_tensor", out=modrow[0:3, c0:c0 + 512], in0=psb[bank][0:3, :],
                                     in1=bmr3[0:3, c0:c0 + 512], op=ALU.add),
                          reads=[("bmr3", mm)], writes=[pk[bank], ("modrow", mm)])
            for mm, m in enumerate((0, 1, 3, 4)):
                for kc in range(8):
                    P.add("pe", OPC("transpose", out=psb[6][:, (mm * 8 + kc) * 3:(mm * 8 + kc) * 3 + 3],
                                    in_=modrow[0:3, mm * D + kc * 128:mm * D + (kc + 1) * 128], identity=consts[0:3, 0, 0:3]),
                          reads=[("modrow", mm), "consts"], writes=[pk[6]], group=("m0t", mm))
                P.add("dve", OPC("tensor_copy", out=modc[:, m * 8:(m + 1) * 8, :],
                                 in_=psb[6][:, mm * 24:(mm + 1) * 24].rearrange("p (k c) -> p k c", k=8)),
                      reads=[], writes=[pk[6], "modc"])
            for (gc, mi, cn) in ((gcol1, 1, C_N1), (gcol2, 4, C_N2)):
                P.add("dve", OPC("tensor_scalar", out=gc[:], in0=modc[:, mi * 8:(mi + 1) * 8, :], scalar1=1.0, scalar2=None, op0=ALU.add),
                    reads=["modc"], writes=["gcol"])
                P.add("dve", OPC("tensor_tensor", out=gc[:], in0=gc[:], in1=bc(cols, cn, [[1, 8], [0, 3]]), op=ALU.mult),
                    reads=["gcol", "cols"], writes=["gcol"])
            P.barrier(dummy[:])

        for b in range(NB):
            with phase() as ph:
                xnT = sb("xnT", [128, 8, SEQ], BF16, ph)
                xin = [sb("xin%d" % i, [128, D], F32, ph) for i in range(2)]
                xsb = [sb("xsb%d" % i, [128, D], BF16, ph) for i in range(2)]
                junk = sb("junk", [128, D], F32, ph)
                st1 = sb("st1", [128, 8], F32, ph)
                st2 = sb("st2", [128, 8], F32, ph)
                wbf = [sb("wbf%d" % i, [128, 8, 1024], BF16, ph) for i in range(3)]
                Gs = sb("Gs", [128, 16], F32, ph)
                sp_ = sb("sp_", [128, 2, 4], F32, ph)
                uu = sb("uu", [128, 2, 4], F32, ph)
                VPt = [sb("VPt%d" % i, [128, 2, 4, 257], BF16, ph) for i in range(2)]
                acc = [sb("acc%d" % i, [128, 384], F32, ph) for i in range(4)]
                ob = [sb("ob%d" % i, [128, 8, 512], BF16, ph) for i in range(2)]
                sot = [sb("sot%d" % i, [128, D], BF16, ph) for i in range(2)]
                gv = [sb("gv%d" % i, [128, D], F32, ph) for i in range(2)]
                vn = [sb("vn%d" % i, [128, D], BF16, ph) for i in range(2)]
                groups = ["v", "k", "q", "o", "u", "vg", "ga", "gb"]

                def load_w(gidx):
                    for hb in range(2):
                        load_blk(gidx, hb)

                def load_blk(gidx, hb):
                    name = groups[gidx]
                    wi = gidx % 3
                    P.dma(wbf[wi][:, :, hb * 512:(hb + 1) * 512], wall_d[WB[name] + hb], writes=[("wbf", wi, hb)], eng="pool")

                def wk(wi):
                    return [("wbf", wi, 0), ("wbf", wi, 1)]

                def ln1_stats(c):
                    xi = xin[c % 2]
                    xk = "xin%d" % (c % 2)
                    o4 = 4 * (c % 2)
                    sk_ = "st1_%d" % (c % 2)
                    src = x_d[b, c * 128:(c + 1) * 128, :] if c < 16 else ctx_d[b, (c - 16) * 128:(c - 15) * 128, :]
                    P.dma(xi[:], src, writes=[xk])
                    P.add("act", OPC("activation", out=junk[:], in_=xi[:], func=AF.Square, accum_out=st1[:, o4:o4 + 1]),
                          reads=[xk], writes=["junk", sk_])
                    P.add("act", OPC("activation", out=st1[:, o4 + 1:o4 + 2], in_=st1[:, o4:o4 + 1], func=AF.Sqrt,
                                     scale=1.0 / D, bias=EPS), reads=[sk_], writes=[sk_ + "b"])
                    P.add("dve", OPC("reciprocal", out=st1[:, o4 + 2:o4 + 3], in_=st1[:, o4 + 1:o4 + 2]),
                          reads=[sk_ + "b"], writes=[sk_ + "c"])
                    P.add("dve", OPC("tensor_scalar", out=xsb[c % 2][:], in0=xi[:], scalar1=st1[:, o4 + 2:o4 + 3], scalar2=None,
                                     op0=ALU.mult), reads=[xk, sk_ + "c"], writes=["xsb%d" % (c % 2)])

                def ln1_tr(c):
                    mcol = b if c < 16 else 2
                    xs = xsb[c % 2]
                    xsk = "xsb%d" % (c % 2)
                    bks = (2 * (c % 2), 2 * (c % 2) + 1)
                    pTs = [psb[bk][:, :].bitcast(BF16) for bk in bks]
                    for kc in range(8):
                        hf = kc % 2
                        P.add("pe", OPC("transpose", out=pTs[hf][:, (kc // 2) * 128:(kc // 2 + 1) * 128],
                                        in_=xs[:, kc * 128:(kc + 1) * 128], identity=identb[:]),
                              reads=[xsk, "identb"], writes=[pk[bks[hf]]], group=("tr1", b, c, hf))
                    for kc in range(0, 8, 2):
                        P.add("act", OPC("activation", out=xnT[:, kc, c * 128:(c + 1) * 128],
                                         in_=pTs[0][:, (kc // 2) * 128:(kc // 2 + 1) * 128], func=AF.Identity,
                                         scale=gcol1[:, kc, mcol:mcol + 1], bias=modc[:, kc, mcol:mcol + 1]),
                              reads=[], writes=[pk[bks[0]], ("xnTa", c)], group=("ev1a", b, c))
                    for kc in range(1, 8, 2):
                        P.add("dve", OPC("tensor_scalar", out=xnT[:, kc, c * 128:(c + 1) * 128],
                                         in0=pTs[1][:, (kc // 2) * 128:(kc // 2 + 1) * 128],
                                         scalar1=gcol1[:, kc, mcol:mcol + 1], scalar2=modc[:, kc, mcol:mcol + 1],
                                         op0=ALU.mult, op1=ALU.add),
                              reads=[], writes=[pk[bks[1]], ("xnTb", c)], group=("ev1b", b, c))

                wl = {2: (0, 0), 5: (0, 1), 8: (1, 0), 11: (1, 1)}
                ln1_stats(0)
                for c in range(NCH):
                    if c + 1 < NCH:
                        ln1_stats(c + 1)
                    ln1_tr(c)
                    if c in wl:
                        load_blk(*wl[c])

                def xr(c):
                    return [("xnTa", c), ("xnTb", c)]
                xr_all = [k_ for c in range(NCH) for k_ in xr(c)]
                def gate_a(c):
                    gi = b * NCH + c
                    bank = 4 + (c % 2)
                    for kc in range(8):
                        P.add("pe", OPC("matmul", psb[bank][:, 0:16], lhsT=xnT[:, kc, c * 128:(c + 1) * 128], rhs=wgb[:, kc, :],
                                        start=(kc == 0), stop=(kc == 7)),
                              reads=xr(c) + ["wgb"], writes=[pk[bank]], group=("g", b, c))
                    P.add("dve", OPC("tensor_tensor", out=Gs[:], in0=psb[bank][:, 0:16], in1=rowsb[:, R_BG:R_BG + 16], op=ALU.add),
                          reads=["rowsb"], writes=[pk[bank], "Gs"])
                    P.add("act", OPC("activation", out=sp_[:], in_=bc(Gs, 4, [[8, 2], [1, 4]]), func=AF.Exp, scale=-1.0),
                          reads=["Gs"], writes=["sp_"])
                    P.add("act", OPC("activation", out=sp_[:], in_=sp_[:], func=AF.Ln, bias=1.0), reads=["sp_"], writes=["sp_"])

                def gate_b(c):
                    gi = b * NCH + c
                    bank2 = 6 + (c % 2)
                    P.add("pe", OPC("matmul", psb[bank2][:, 0:4], lhsT=triU, rhs=sp_[:, 0, :], start=True, stop=True),
                          reads=["sp_", "consts"], writes=[pk[bank2]], group=("cs", b, c))
                    P.add("pe", OPC("matmul", psb[bank2][:, 4:8], lhsT=triL, rhs=sp_[:, 1, :], start=True, stop=True),
                          reads=["sp_", "consts"], writes=[pk[bank2]], group=("cs", b, c))
                    P.add("pe", OPC("matmul", psb[bank2][:, 8:16], lhsT=ones, rhs=bc(sp_, 0, [[1, 8]]), start=True, stop=True),
                          reads=["sp_", "consts"], writes=[pk[bank2]], group=("cs", b, c))
                    P.add("dve", OPC("tensor_tensor", out=uu[:], in0=bc(psb[bank2], 0, [[4, 2], [1, 4]]),
                                     in1=bc(Gs, 0, [[8, 2], [1, 4]]), op=ALU.add), reads=["Gs"], writes=[pk[bank2], "uu"])
                    P.add("act", OPC("activation", out=GATES[:, gi, 0:8], in_=bc(uu, 0, [[1, 8]]), func=AF.Exp),
                          reads=["uu"], writes=[("G", gi)], group=("gw", gi))
                    P.add("act", OPC("activation", out=GATES[:, gi, 8:16], in_=psb[bank2][:, 0:8], func=AF.Exp),
                          reads=[], writes=[pk[bank2], ("G", gi)], group=("gw", gi))
                    P.add("act", OPC("activation", out=GATES[:, gi, 16:24], in_=psb[bank2][:, 8:16], func=AF.Exp, scale=-1.0),
                          reads=[], writes=[pk[bank2], ("G", gi)], group=("gw", gi))
                load_w(2)
                gate_a(0)
                gate_b(0)
                for c in range(NCH):
                    if c + 1 < NCH:
                        gate_a(c + 1)
                    gi = b * NCH + c
                    vp = VPt[c % 2]
                    vk = "VPt%d" % (c % 2)
                    for hb in range(2):
                        bank = (2 * c + hb) % 4
                        for kc in range(8):
                            P.add("pe", OPC("matmul", psb[bank][:, :], lhsT=xnT[:, kc, c * 128:(c + 1) * 128],
                                            rhs=wbf[0][:, kc, hb * 512:(hb + 1) * 512], start=(kc == 0), stop=(kc == 7)),
                                  reads=xr(c) + [("wbf", 0, hb)], writes=[pk[bank]], group=("v", b, c, hb))
                        for d_ in range(2):
                            P.add("dve", OPC("tensor_tensor", out=vp[:, d_, 2 * hb:2 * hb + 2, 0:256],
                                             in0=bc(psb[bank], 0, [[256, 2], [1, 256]]),
                                             in1=bc(GATES, gi * 24 + d_ * 4 + 2 * hb, [[1, 2], [0, 256]]), op=ALU.mult),
                                  reads=[("G", gi)], writes=[pk[bank], vk], group=("vev", gi, hb))
                    P.add("act", OPC("activation", out=vp[:, :, :, 256], in_=bc(GATES, gi * 24, [[4, 2], [1, 4]]), func=AF.Copy),
                          reads=[("G", gi)], writes=[vk])
                    P.dma(VP_d[gi], vp[:].rearrange("p a h v -> p (a h v)"), reads=[vk], writes=[("VPd", gi)], eng="pool")
                    if c + 1 < NCH:
                        gate_b(c + 1)
                load_w(3)

                def qk_phase(name, wi, dst_d, cbase):
                    it = 0
                    ti = 0
                    pend = []
                    for (seq0, L) in ((0, S), (S, CL)):
                        for (t0, n) in seq_tiles(L):
                            a = max(t0 - 1, 0)
                            bnd = min(t0 + n + 1, L)
                            nw = bnd - a
                            c0 = t0 - a
                            lo = 1 if t0 == 0 else 0
                            hi = n - 1 if t0 + n == L else n
                            q_ = ob[ti % 2]
                            qk_ = "ob%d" % (ti % 2)
                            ti += 1
                            for cp in range(4):
                                ccs = (2 * cp, 2 * cp + 1)
                                bks = [(it + i) % 8 for i in range(2)]
                                acs = [acc[(it + i) % 4] for i in range(2)]
                                aks = ["acc%d" % ((it + i) % 4) for i in range(2)]
                                it += 2
                                for i, cc in enumerate(ccs):
                                    for kc in range(8):
                                        P.add("pe", OPC("matmul", psb[bks[i]][:, 0:nw], lhsT=wbf[wi][:, kc, cc * 128:(cc + 1) * 128],
                                                        rhs=xnT[:, kc, seq0 + a:seq0 + a + nw], start=(kc == 0), stop=(kc == 7)),
                                              reads=xr_all + [("wbf", wi, cc // 4)], writes=[pk[bks[i]]], group=("qk", name, it, i))
                                cws = [C_CONV + (cbase + cc) * 3 for cc in ccs]
                                for i in range(2):
                                    P.add("act", OPC("activation", out=acs[i][:, 0:n], in_=psb[bks[i]][:, c0:c0 + n], func=AF.Copy,
                                                     scale=cols[:, cws[i] + 1:cws[i] + 2]),
                                          reads=["cols"], writes=[pk[bks[i]], aks[i]])
                                for i in range(2):
                                    P.add("dve", OPC("scalar_tensor_tensor", out=acs[i][:, lo:n], in0=psb[bks[i]][:, c0 - 1 + lo:c0 - 1 + n],
                                                     scalar=cols[:, cws[i]:cws[i] + 1], in1=acs[i][:, lo:n], op0=ALU.mult, op1=ALU.add),
                                          reads=[aks[i], "cols"], writes=[pk[bks[i]], aks[i]])
                                for i in range(2):
                                    P.add("dve", OPC("scalar_tensor_tensor", out=acs[i][:, 0:hi], in0=psb[bks[i]][:, c0 + 1:c0 + 1 + hi],
                                                     scalar=cols[:, cws[i] + 2:cws[i] + 3], in1=acs[i][:, 0:hi], op0=ALU.mult, op1=ALU.add),
                                          reads=[aks[i], "cols"], writes=[pk[bks[i]], aks[i]])
                                for fn_ in pend:
                                    fn_()
                                del pend[:]
                                for i, cc in enumerate(ccs):
                                    pend.append(lambda i=i, cc=cc, acs=acs, aks=aks, q_=q_, qk_=qk_, n=n, ti=ti: P.add(
                                        "act", OPC("activation", out=q_[:, cc, 0:n], in_=acs[i][:, 0:n], func=AF.Silu),
                                        reads=[aks[i]], writes=[qk_], group=("qo", name, ti)))
                            for fn_ in pend:
                                fn_()
                            del pend[:]
                            for jj in range(n // 128):
                                ch = b * NCH + (seq0 + t0) // 128 + jj
                                P.dma(dst_d[ch], q_[:, :, jj * 128:(jj + 1) * 128], reads=[qk_], writes=[(name, ch)], eng="pool")

                qk_phase("KTd", 1, KT_d, 8)
                load_w(4)
                qk_phase("QTd", 2, QT_d, 0)
                load_w(5)

                for c in range(16):
                    so = sot[c % 2]
                    sk = "sot%d" % (c % 2)
                    for hb in range(2):
                        bank = (2 * c + hb) % 4
                        for kc in range(8):
                            P.add("pe", OPC("matmul", psb[bank][:, :], lhsT=xnT[:, kc, c * 128:(c + 1) * 128],
                                            rhs=wbf[0][:, kc, hb * 512:(hb + 1) * 512], start=(kc == 0), stop=(kc == 7)),
                                  reads=xr(c) + [("wbf", 0, hb)], writes=[pk[bank]], group=("o", b, c, hb))
                        P.add("act", OPC("activation", out=so[:, hb * 512:(hb + 1) * 512], in_=psb[bank][:, :], func=AF.Sigmoid),
                              reads=[], writes=[pk[bank], sk], group=("so", b, c))
                    P.dma(SO_d[b * 16 + c], so[:], reads=[sk], writes=[("SOd", b * 16 + c)], eng="pool")
                load_w(6)

                fmc = [0]

                def fm_phase(name, wi, func, dst_d):
                    it = 4
                    for t in range(4):
                        f_ = ob[fmc[0] % 2]
                        fk = "ob%d" % (fmc[0] % 2)
                        fmc[0] += 1
                        for cc in range(8):
                            bank = 4 + it % 4
                            it += 1
                            for kc in range(8):
                                P.add("pe", OPC("matmul", psb[bank][:, :], lhsT=wbf[wi][:, kc, cc * 128:(cc + 1) * 128],
                                                rhs=xnT[:, kc, t * 512:(t + 1) * 512], start=(kc == 0), stop=(kc == 7)),
                                      reads=xr_all + [("wbf", wi, cc // 4)], writes=[pk[bank]], group=("fm", name, t, cc))
                            P.add("act", OPC("activation", out=f_[:, cc, :], in_=psb[bank][:, :], func=func),
                                  reads=[], writes=[pk[bank], fk], group=("ft", name, t))
                        P.dma(dst_d[b * 4 + t], f_[:], reads=[fk], writes=[(name, b * 4 + t)], eng="pool")

                fm_phase("UTd", 1, AF.Gelu_apprx_tanh, UT_d)
                load_w(7)
                def vg_a(c):
                    g_ = gv[c % 2]
                    gk = "gv%d" % (c % 2)
                    o4 = 0 if c % 2 == 0 else 3
                    sk_ = "st2_%d" % (c % 2)
                    for hb in range(2):
                        bank = (2 * c + hb) % 4
                        for kc in range(8):
                            P.add("pe", OPC("matmul", psb[bank][:, :], lhsT=xnT[:, kc, c * 128:(c + 1) * 128],
                                            rhs=wbf[2][:, kc, hb * 512:(hb + 1) * 512], start=(kc == 0), stop=(kc == 7)),
                                  reads=xr(c) + [("wbf", 2, hb)], writes=[pk[bank]], group=("vg", b, c, hb))
                        P.add("act", OPC("activation", out=g_[:, hb * 512:(hb + 1) * 512], in_=psb[bank][:, :], func=AF.Gelu_apprx_tanh),
                              reads=[], writes=[pk[bank], gk], group=("gv", b, c))
                    if c % 3 == 2:
                        P.add("act", OPC("activation", out=junk[:], in_=g_[:], func=AF.Square, accum_out=st2[:, o4:o4 + 1]),
                              reads=[gk], writes=["junk", sk_])
                    else:
                        P.add("dve", OPC("tensor_tensor", out=junk[:], in0=g_[:], in1=g_[:], op=ALU.mult),
                              reads=[gk], writes=["junk"])
                        P.add("dve", OPC("tensor_reduce", out=st2[:, o4:o4 + 1], in_=junk[:], axis=mybir.AxisListType.X, op=ALU.add),
                              reads=["junk"], writes=[sk_])

                def vg_b(c):
                    g_ = gv[c % 2]
                    gk = "gv%d" % (c % 2)
                    v_ = vn[c % 2]
                    vk = "vn%d" % (c % 2)
                    o4 = 0 if c % 2 == 0 else 3
                    sk_ = "st2_%d" % (c % 2)
                    P.add("act", OPC("activation", out=st2[:, o4 + 1:o4 + 2], in_=st2[:, o4:o4 + 1], func=AF.Sqrt, scale=1.0 / D, bias=EPS),
                          reads=[sk_], writes=[sk_ + "b"])
                    P.add("dve", OPC("reciprocal", out=st2[:, o4 + 2:o4 + 3], in_=st2[:, o4 + 1:o4 + 2]), reads=[sk_ + "b"], writes=[sk_ + "c"])
                    P.add("dve", OPC("scalar_tensor_tensor", out=v_[:], in0=g_[:], scalar=st2[:, o4 + 2:o4 + 3], in1=rowsb[:, R_GSGU:R_GSGU + D],
                                     op0=ALU.mult, op1=ALU.mult), reads=[gk, sk_ + "c", "rowsb"], writes=[vk])
                    P.dma(VN_d[b * 16 + c], v_[:], reads=[vk], writes=[("VNd", b * 16 + c)], eng="pool")

                vg_a(0)
                for c in range(16):
                    if c + 1 < 16:
                        vg_a(c + 1)
                    vg_b(c)
                fm_phase("SGAd", 0, AF.Sigmoid, SGA_d)
                fm_phase("SGBd", 1, AF.Sigmoid, SGB_d)
                P.barrier(dummy[:])

        ph_outer = phase()
        ph_outer.__enter__()
        wres = [sb("wres%d" % i, [128, 8, 1024], BF16) for i in range(3)]
        with phase() as ph:
            SW = []
            for b in range(NB):
                d = dict(
                    kTc=[sb("kTc%d" % i, [128, 8, 128], BF16, ph) for i in range(2)],
                    qTc=[sb("qTc%d" % i, [128, 8, 128], BF16, ph) for i in range(2)],
                    VPc=[sb("VPc%d" % i, [128, 2, 4, 257], BF16, ph) for i in range(2)],
                    CBc=[sb("CBc%d" % i, [128, 4, 2, 257], BF16, ph) for i in range(2)],
                    SOc=[sb("SOc%d" % i, [128, D], BF16, ph) for i in range(2)],
                    Ktok=[sb("Ktok%d" % i, [128, D], BF16, ph) for i in range(2)],
                    Z=sb("Z", [128, 4, 2, 257], F32, ph),
                    Cf=[sb("Cf%d" % i, [128, 4, 2, 257], BF16, ph) for i in range(2)],
                    aT=sb("aT", [128, 2, 4, 128], BF16, ph),
                    hh=sb("hh", [128, D], F32, ph),
                    hm=sb("hm", [128, D], BF16, ph),
                    hmT=[sb("hmT%d" % i, [128, 8, 128], BF16, ph) for i in range(1)],
                    sm=sb("sm", [128, 16], F32, ph),
                    junk2=sb("junk2", [128, 256], F32, ph),
                    cnt=0, prev=None,
                    banks=((0, 2, 4, 5) if b == 0 else (1, 3, 6, 7)),
                )
                SW.append(d)

            def chunk_step(b, c, d_, first, last, mode):
                w = SW[b]
                K = lambda s_: (s_, b)
                bA, bB, bC, bD = w["banks"]
                i2 = w["cnt"] % 2
                w["cnt"] += 1
                gi = b * NCH + c
                lat = c < 16
                Z, aT, hh, hm, sm, junk2 = w["Z"], w["aT"], w["hh"], w["hm"], w["sm"], w["junk2"]
                zks = [K(("Z", h, hf)) for h in range(4) for hf in range(2)]
                kt, ktk = w["kTc"][i2], K("kTc%d" % i2)
                vp, vpk = w["VPc"][i2], K("VPc%d" % i2)
                P.dma(kt[:], KT_d[gi], reads=[("KTd", gi)], writes=[ktk])
                if mode == "full":
                    P.dma(vp[:].rearrange("p a h v -> p (a h v)"), VP_d[gi], reads=[("VPd", gi)], writes=[vpk])
                else:
                    P.dma(vp[:, d_].rearrange("p h v -> p (h v)"), VP_d[gi][:, d_ * 1028:(d_ + 1) * 1028],
                          reads=[("VPd", gi)], writes=[vpk])
                if mode == "full":
                    qt, qtk = w["qTc"][i2], K("qTc%d" % i2)
                    cb, cbk = w["CBc"][i2], K("CBc%d" % i2)
                    so, sok = w["SOc"][i2], K("SOc%d" % i2)
                    P.dma(qt[:], QT_d[gi], reads=[("QTd", gi)], writes=[qtk])
                    P.dma(cb[:].rearrange("p h a v -> p (h a v)"), CB_d[b * 16 + c], reads=[("CBd", b * 16 + c)], writes=[cbk])
                    P.dma(so[:], SO_d[b * 16 + c], reads=[("SOd", b * 16 + c)], writes=[sok])
                ktok, ktokk = w["Ktok"][i2], K("Ktok%d" % i2)
                yield
                if not last:
                    pT = psb[bA][:, :].bitcast(BF16)
                    for cc in range(8):
                        P.add("pe", OPC("transpose", out=pT[:, cc * 128:(cc + 1) * 128], in_=kt[:, cc, :], identity=identb[:]),
                              reads=[ktk, "identb"], writes=[pk[bA]], group=("trk", gi, d_))
                    P.add("act", OPC("activation", out=ktok[:], in_=pT, func=AF.Copy, scale=0.0625),
                          reads=[], writes=[pk[bA], ktokk])
                need_c = (mode == "full") or (d_ == 1 and lat)
                Cf, cfk = w["Cf"][i2], K("Cf%d" % i2)
                gp = w["prev"]
                if need_c:
                    if first:
                        P.add("pool", OPC("memset", Cf[:], 0.0), writes=[cfk])
                    else:
                        P.add("pool", OPC("tensor_tensor", out=Cf[:, 0:2].rearrange("p h a v -> p h (a v)"),
                                          in0=Z[:, 0:2].rearrange("p h a v -> p h (a v)"),
                                          in1=bc(GATES, gp * 24 + 16 + d_ * 4, [[1, 2], [0, 514]]), op=ALU.mult),
                              reads=zks + [("G", gp)], writes=[cfk])
                        for h in (2, 3):
                            P.add("act", OPC("activation", out=Cf[:, h].rearrange("p a v -> p (a v)"),
                                             in_=Z[:, h].rearrange("p a v -> p (a v)"), func=AF.Copy,
                                             scale=GATES[:, gp, 16 + d_ * 4 + h:17 + d_ * 4 + h]),
                                  reads=zks + [("G", gp)], writes=[cfk])
                    if d_ == 1:
                        P.dma(CB_d[b * 16 + c], Cf[:].rearrange("p h a v -> p (h a v)"), reads=[cfk],
                              writes=[("CBd", b * 16 + c)], eng="pool")
                yield
                if mode == "full":
                    for h in range(4):
                        for hf in range(2):
                            P.add("pe", OPC("matmul", psb[bB][:, h * 128:(h + 1) * 128], lhsT=kt[:, 2 * h + hf, :],
                                            rhs=qt[:, 2 * h + hf, :], start=(hf == 0), stop=(hf == 1)),
                                  reads=[ktk, qtk], writes=[pk[bB]], group=("S", gi, h))
                    for dd, msk in ((0, mF), (1, mB)):
                        P.add("dve", OPC("tensor_tensor", out=aT[:, dd, :, :], in0=bc(psb[bB], 0, [[128, 4], [1, 128]]),
                                         in1=bc(msk, 0, [[0, 4], [1, 128]]), op=ALU.mult),
                              reads=["mF", "mB"], writes=[pk[bB], K("aT")], group=("aT", gi))
                    yield
                    for h in range(4):
                        for dd in range(2):
                            cst = Cf if dd == 0 else cb
                            cstk = cfk if dd == 0 else cbk
                            col = dd * 4 + h
                            P.add("pe", OPC("matmul", psb[bB][:, col:col + 1], lhsT=aT[:, dd, h, :], rhs=vp[:, dd, h, 256:257],
                                            start=True, stop=False),
                                  reads=[K("aT"), vpk], writes=[pk[bB]], group=("den", gi, h, dd))
                            for hf in range(2):
                                P.add("pe", OPC("matmul", psb[bB][:, col:col + 1], lhsT=qt[:, 2 * h + hf, :], rhs=cst[:, h, hf, 256:257],
                                                start=False, stop=(hf == 1)),
                                      reads=[qtk, cstk], writes=[pk[bB]], group=("den", gi, h, dd))
                    smk_all = [K(("sm", h, dd)) for h in range(4) for dd in range(2)]
                    smr_all = [K(("smr", h)) for h in range(4)]
                    P.add("act", OPC("activation", out=sm[:, 0:8], in_=psb[bB][:, 0:8], func=AF.Abs),
                          reads=[], writes=[pk[bB]] + smk_all)
                    P.add("dve", OPC("tensor_tensor", out=sm[:, 8:16], in0=sm[:, 0:8], in1=GATES[:, gi, 8:16], op=ALU.max),
                          reads=smk_all + [("G", gi)], writes=smr_all)
                    P.add("dve", OPC("reciprocal", out=sm[:, 8:16], in_=sm[:, 8:16]), reads=smr_all, writes=smr_all)
                    yield
                    for h in range(4):
                        banks = (bC, bD) if h % 2 == 0 else (bA, bB)
                        for dd in range(2):
                            bk = banks[dd]
                            cst = Cf if dd == 0 else cb
                            cstk = cfk if dd == 0 else cbk
                            P.add("pe", OPC("matmul", psb[bk][:, 0:256], lhsT=aT[:, dd, h, :], rhs=vp[:, dd, h, 0:256],
                                            start=True, stop=False),
                                  reads=[K("aT"), vpk], writes=[pk[bk]], group=("N", gi, h, dd))
                            for hf in range(2):
                                P.add("pe", OPC("matmul", psb[bk][:, 0:256], lhsT=qt[:, 2 * h + hf, :], rhs=cst[:, h, hf, 0:256],
                                                start=False, stop=(hf == 1)),
                                      reads=[qtk, cstk], writes=[pk[bk]], group=("N", gi, h, dd))
                        P.add("act", OPC("activation", out=hh[:, h * 256:(h + 1) * 256], in_=psb[banks[0]][:, 0:256], func=AF.Copy,
                                         scale=sm[:, 8 + h:9 + h]),
                              reads=[K(("smr", h))], writes=[pk[banks[0]], K(("hh", h))])
                        P.add("dve", OPC("scalar_tensor_tensor", out=hh[:, h * 256:(h + 1) * 256], in0=psb[banks[1]][:, 0:256],
                                         scalar=sm[:, 12 + h:13 + h], in1=hh[:, h * 256:(h + 1) * 256], op0=ALU.mult, op1=ALU.add),
                              reads=[K(("smr", h)), K(("hh", h))], writes=[pk[banks[1]], K(("hh", h))])
                        if h % 2 == 1:
                            yield
                if not last:
                    for h in range(4):
                        for hf in range(2):
                            bk = (bA, bB, bC, bD)[(2 * h + hf) % 4]
                            P.add("pe", OPC("matmul", psb[bk][:, 0:257], lhsT=ktok[:, h * 256 + hf * 128:h * 256 + (hf + 1) * 128],
                                            rhs=vp[:, d_, h, :], start=True, stop=True),
                                  reads=[ktokk, vpk], writes=[pk[bk]], group=("U", gi, d_, h, hf))
                            zk = K(("Z", h, hf))
                            if first:
                                P.add("act", OPC("activation", out=Z[:, h, hf, :], in_=psb[bk][:, 0:257], func=AF.Copy),
                                      reads=[], writes=[pk[bk], zk])
                            else:
                                P.add("dve", OPC("scalar_tensor_tensor", out=Z[:, h, hf, :], in0=Z[:, h, hf, :],
                                                 scalar=GATES[:, gp, 16 + d_ * 4 + h:17 + d_ * 4 + h], in1=psb[bk][:, 0:257],
                                                 op0=ALU.mult, op1=ALU.add),
                                      reads=[zk, ("G", gp)], writes=[pk[bk], zk])
                        yield
                w["prev"] = gi
                if mode == "full":
                    hks = [K(("hh", h)) for h in range(4)]
                    P.add("dve", OPC("tensor_tensor", out=hh[:], in0=hh[:], in1=so[:], op=ALU.mult),
                          reads=hks + [sok], writes=hks)
                    yield
                    for h in range(4):
                        P.add("act", OPC("activation", out=junk2[:], in_=hh[:, h * 256:(h + 1) * 256], func=AF.Square,
                                         accum_out=sm[:, h:h + 1]),
                              reads=[K(("hh", h))], writes=[K("junk2"), K(("sm", h, 0))])
                    s4 = [K(("sm", h, 0)) for h in range(4)]
                    s5 = [K(("sm", h, 1)) for h in range(4)]
                    P.add("act", OPC("activation", out=sm[:, 4:8], in_=sm[:, 0:4], func=AF.Sqrt, scale=1.0 / 256, bias=EPS),
                          reads=s4, writes=s5)
                    yield
                    P.add("dve", OPC("reciprocal", out=sm[:, 4:8], in_=sm[:, 4:8]), reads=s5, writes=s5)
                    P.add("dve", OPC("tensor_tensor", out=hm[:].rearrange("p (h v) -> p h v", h=4),
                                     in0=hh[:].rearrange("p (h v) -> p h v", h=4), in1=bc(sm, 4, [[1, 4], [0, 256]]), op=ALU.mult),
                          reads=hks + s5, writes=[K("hm")])
                    yield
                    pT = psb[bA][:, :].bitcast(BF16)
                    for kc in range(8):
                        P.add("pe", OPC("transpose", out=pT[:, kc * 128:(kc + 1) * 128], in_=hm[:, kc * 128:(kc + 1) * 128],
                                        identity=identb[:]),
                              reads=[K("hm"), "identb"], writes=[pk[bA]], group=("trh", gi))
                    yield
                    ht, htk = w["hmT"][0], K("hmT0")
                    P.add("dve", OPC("tensor_tensor", out=ht[:, 0:4, :], in0=pT[:, 0:512].rearrange("p (k t) -> p k t", k=4),
                                     in1=bc(cols, C_GMH, [[1, 4], [0, 128]]), op=ALU.mult),
                          reads=["cols"], writes=[pk[bA], htk])
                    for kc in range(4, 8):
                        P.add("act", OPC("activation", out=ht[:, kc, :], in_=pT[:, kc * 128:(kc + 1) * 128], func=AF.Copy,
                                         scale=cols[:, C_GMH + kc:C_GMH + kc + 1]),
                              reads=["cols"], writes=[pk[bA], htk], group=("hte", gi))
                    P.dma(HMT_d[b * 16 + c], ht[:], reads=[htk], writes=[("HMTd", b * 16 + c)], eng="pool")

            order_b = [17, 16] + list(range(15, -1, -1))
            order_f = [16, 17] + list(range(16))
            def drive(gens):
                gens = list(gens)
                while gens:
                    for g_ in list(gens):
                        try:
                            next(g_)
                        except StopIteration:
                            gens.remove(g_)

            for n_, c in enumerate(order_b):
                drive(chunk_step(b, c, 1, first=(n_ == 0), last=(n_ == len(order_b) - 1), mode="state")
                      for b in range(NB))
            for wi, name in enumerate(("wa", "wb", "wout")):
                for hb in range(2):
                    P.dma(wres[wi][:, :, hb * 512:(hb + 1) * 512], wall_d[WB[name] + hb], writes=[("wres", wi, hb)], eng="pool")
            for n_, c in enumerate(order_f):
                drive(chunk_step(b, c, 0, first=(n_ == 0), last=(n_ == len(order_f) - 1),
                                 mode=("full" if c < 16 else "state")) for b in range(NB))
            P.barrier(dummy[:])

        with phase() as ph:
            TT = 256
            hmTt = [sb("hmTt%d" % i, [128, 8, TT], BF16, ph) for i in range(2)]
            sga = [sb("sga%d" % i, [128, 8, TT], BF16, ph) for i in range(2)]
            sgb = [sb("sgb%d" % i, [128, 8, TT], BF16, ph) for i in range(2)]
            ut = [sb("ut%d" % i, [128, 8, TT], BF16, ph) for i in range(2)]
            vnt = [sb("vnt%d" % i, [128, 2, D], BF16, ph) for i in range(2)]
            xtb = [[sb("xt%d_%d" % (i, j), [128, D], F32, ph) for j in range(2)] for i in range(2)]
            y1 = [sb("y1_%d" % i, [128, 8, TT], F32, ph) for i in range(2)]
            gat = [sb("gat%d" % i, [128, 8, TT], BF16, ph) for i in range(2)]
            yT = [sb("yT%d" % i, [128, 8, TT], BF16, ph) for i in range(2)]
            tmp = [[sb("tmp%d_%d" % (i, j), [128, 512], F32, ph) for j in range(2)] for i in range(2)]
            xs2 = [sb("xs2%d" % i, [128, D], BF16, ph) for i in range(2)]
            xn2 = [sb("xn2%d" % i, [128, 8, TT], BF16, ph) for i in range(2)]
            junk3 = [sb("junk3%d" % i, [128, D], BF16, ph) for i in range(2)]
            tq = [[sb("tq%d_%d" % (i, j), [128, TT], F32, ph) for j in range(4)] for i in range(2)]
            st3 = sb("st3", [128, 16], F32, ph)
            ones1 = consts[0:1, 3, :]

            def mix_tile(T2):
                b = T2 // 8
                t = (T2 % 8) // 2
                hf2 = T2 % 2
                T0 = b * 4 + t
                p2 = T2 % 2
                itl = [0]

                def nbank():
                    bk = 4 * p2 + itl[0] % 4
                    itl[0] += 1
                    return bk
                hm_, sga_, sgb_, ut_, vn_ = hmTt[p2], sga[p2], sgb[p2], ut[p2], vnt[p2]
                y1_, gat_, yT_, xn_ = y1[p2], gat[p2], yT[p2], xn2[p2]
                kk = lambda n_: (n_, p2)
                for j in range(2):
                    ch = T2 * 2 + j
                    P.dma(hm_[:, :, j * 128:(j + 1) * 128], HMT_d[ch], reads=[("HMTd", ch)], writes=[kk(("hmTt", j))])
                    P.dma(vn_[:, j, :], VN_d[ch], reads=[("VNd", ch)], writes=[kk(("vnt", j))])
                P.dma(sga_[:], SGA_d[T0][:, :, hf2 * TT:(hf2 + 1) * TT], reads=[("SGAd", T0)], writes=[kk("sga")])
                P.dma(ut_[:], UT_d[T0][:, :, hf2 * TT:(hf2 + 1) * TT], reads=[("UTd", T0)], writes=[kk("ut")])
                P.dma(sgb_[:], SGB_d[T0][:, :, hf2 * TT:(hf2 + 1) * TT], reads=[("SGBd", T0)], writes=[kk("sgb")])
                for j in range(2):
                    ch = T2 * 2 + j
                    cl = ch % 16
                    P.dma(xtb[p2][j][:], x_d[b, cl * 128:(cl + 1) * 128, :], writes=[kk(("xt", j))])
                yield
                hks_ = [kk(("hmTt", j)) for j in range(2)]
                for cc in range(8):
                    bank = nbank()
                    for kc in range(8):
                        P.add("pe", OPC("matmul", psb[bank][:, 0:TT], lhsT=wres[0][:, kc, cc * 128:(cc + 1) * 128], rhs=hm_[:, kc, :],
                                        start=(kc == 0), stop=(kc == 7)),
                              reads=[("wres", 0, cc // 4)] + hks_, writes=[pk[bank]], group=("ya", T2, cc))
                    P.add("dve", OPC("tensor_tensor", out=y1_[:, cc, :], in0=psb[bank][:, 0:TT], in1=sga_[:, cc, :], op=ALU.mult),
                          reads=[kk("sga")], writes=[pk[bank], kk(("y1", cc))])
                    if cc % 4 == 3:
                        yield
                for g0 in range(0, 8, 4):
                    bks_ = []
                    for g in range(g0, g0 + 4):
                        bank = nbank()
                        bks_.append(bank)
                        for j in range(2):
                            P.add("pe", OPC("matmul", psb[bank][:, j * 128:(j + 1) * 128], lhsT=vn_[:, j, g * 128:(g + 1) * 128],
                                            rhs=wsT[:, g, :], start=True, stop=True),
                                  reads=[kk(("vnt", j)), "wsT"], writes=[pk[bank]], group=("sgu", T2, g))
                    for g in range(g0, g0 + 4):
                        bank = bks_[g - g0]
                        P.add("dve", OPC("tensor_tensor", out=tq[p2][g % 4][:].rearrange("p (j q) -> p j q", j=2),
                                         in0=bc(psb[bank], 0, [[128, 2], [1, 128]]),
                                         in1=bc(rowsb, R_BS + g * 128, [[0, 2], [1, 128]]), op=ALU.add),
                              reads=["rowsb"], writes=[pk[bank], kk(("tq", g % 4))])
                    for g in range(g0, g0 + 4):
                        P.add(("dve", "pool")[g % 2], OPC("tensor_tensor", out=gat_[:, g, :], in0=tq[p2][g % 4][:], in1=ut_[:, g, :], op=ALU.mult),
                              reads=[kk(("tq", g % 4)), kk("ut")], writes=[kk(("gat", g))])
                    yield
                gks = [kk(("gat", g)) for g in range(8)]
                for cc in range(8):
                    bank = nbank()
                    for kc in range(8):
                        P.add("pe", OPC("matmul", psb[bank][:, 0:TT], lhsT=wres[1][:, kc, cc * 128:(cc + 1) * 128], rhs=gat_[:, kc, :],
                                        start=(kc == 0), stop=(kc == 7)),
                              reads=[("wres", 1, cc // 4)] + gks, writes=[pk[bank]], group=("yb", T2, cc))
                    tm = tmp[p2][cc % 2]
                    tk = kk("tmp%d" % (cc % 2))
                    P.add("dve", OPC("tensor_tensor", out=tm[:, 0:TT], in0=psb[bank][:, 0:TT], in1=sgb_[:, cc, :], op=ALU.mult),
                          reads=[kk("sgb")], writes=[pk[bank], tk])
                    P.add("pool", OPC("tensor_tensor", out=yT_[:, cc, :], in0=tm[:, 0:TT], in1=y1_[:, cc, :], op=ALU.add),
                          reads=[tk, kk(("y1", cc))], writes=[kk(("yT", cc))])
                    if cc % 4 == 3:
                        yield
                yks = [kk(("yT", cc)) for cc in range(8)]
                for j in range(2):
                    ch = T2 * 2 + j
                    xt_ = xtb[p2][j]
                    xtk = kk(("xt", j))
                    for nb_ in range(2):
                        bank = nbank()
                        for kc in range(8):
                            P.add("pe", OPC("matmul", psb[bank][:, :], lhsT=yT_[:, kc, j * 128:(j + 1) * 128],
                                            rhs=wres[2][:, kc, nb_ * 512:(nb_ + 1) * 512], start=(kc == 0), stop=(kc == 7)),
                                  reads=[("wres", 2, nb_)] + yks, writes=[pk[bank]], group=("mix", T2, j, nb_))
                        tm = tmp[p2][nb_]
                        tk = kk("tmp%d" % nb_)
                        P.add("dve", OPC("tensor_tensor", out=tm[:], in0=psb[bank][:, :], in1=g1bc[:, b, nb_ * 512:(nb_ + 1) * 512], op=ALU.mult),
                              reads=["gbc"], writes=[pk[bank], tk])
                        P.add("pool", OPC("tensor_tensor", out=xt_[:, nb_ * 512:(nb_ + 1) * 512], in0=tm[:], in1=xt_[:, nb_ * 512:(nb_ + 1) * 512],
                                          op=ALU.add), reads=[tk, xtk], writes=[xtk])
                    yield
                    P.dma(X1_d[ch], xt_[:], reads=[xtk], writes=[("X1d", ch)], eng="pool")
                    o4 = 8 * p2 + 4 * j
                    sk_ = "st3_%d" % (o4)
                    P.add("act", OPC("activation", out=junk3[p2][:], in_=xt_[:], func=AF.Square, accum_out=st3[:, o4:o4 + 1]),
                          reads=[xtk], writes=[kk("junk3"), sk_])
                    P.add("act", OPC("activation", out=st3[:, o4 + 1:o4 + 2], in_=st3[:, o4:o4 + 1], func=AF.Sqrt,
                                     scale=1.0 / D, bias=EPS), reads=[sk_], writes=[sk_ + "b"])
                    yield
                    P.add("dve", OPC("reciprocal", out=st3[:, o4 + 2:o4 + 3], in_=st3[:, o4 + 1:o4 + 2]), reads=[sk_ + "b"], writes=[sk_ + "c"])
                    xs = xs2[p2]
                    xsk = kk("xs2")
                    P.add("dve", OPC("tensor_scalar", out=xs[:], in0=xt_[:], scalar1=st3[:, o4 + 2:o4 + 3], scalar2=None, op0=ALU.mult),
                          reads=[xtk, sk_ + "c"], writes=[xsk])
                    yield
                    bank = nbank()
                    pT = psb[bank][:, :].bitcast(BF16)
                    for kc in range(8):
                        P.add("pe", OPC("transpose", out=pT[:, kc * 128:(kc + 1) * 128], in_=xs[:, kc * 128:(kc + 1) * 128], identity=identb[:]),
                              reads=[xsk, "identb"], writes=[pk[bank]], group=("tr2", ch))
                    yield
                    for kc in range(8):
                        P.add("act", OPC("activation", out=xn_[:, kc, j * 128:(j + 1) * 128], in_=pT[:, kc * 128:(kc + 1) * 128],
                                         func=AF.Identity, scale=gcol2[:, kc, b:b + 1], bias=modc[:, 24 + kc, b:b + 1]),
                              reads=["gcol"], writes=[pk[bank], kk("xn2")], group=("ev2", ch))
                P.dma(XN2_d[T2], xn_[:], reads=[kk("xn2")], writes=[("XN2d", T2)], eng="pool")

            def drive2(gens):
                gens = list(gens)
                while gens:
                    for g_ in list(gens):
                        try:
                            next(g_)
                        except StopIteration:
                            gens.remove(g_)

            for T2 in range(0, NB * 8, 2):
                drive2([mix_tile(T2), mix_tile(T2 + 1)])
            P.barrier(dummy[:])

        ph_outer.__exit__()
        ph_outer2 = phase()
        ph_outer2.__enter__()
        w2b = sb("w2b", [128, 32, 1024], BF16)
        with phase() as ph:
            w1b = sb("w1b", [128, 8, 4096], BF16, ph)
            for i in range(8):
                P.dma(w1b[:, :, i * 512:(i + 1) * 512], wall_d[WB["w1"] + i], writes=[("w1b", i)], eng="pool")
            for i in range(8):
                fg, nb_ = i // 2, i % 2
                P.dma(w2b[:, fg * 8:(fg + 1) * 8, nb_ * 512:(nb_ + 1) * 512], wall_d[WB["w2"] + i], writes=[("w2b", i)], eng="pool")
            xn2t = [sb("xn2t%d" % i, [128, 8, 256], BF16, ph) for i in range(2)]
            h1t = [sb("h1t%d" % i, [128, 32, 256], BF16, ph) for i in range(2)]
            sq = [sb("sq%d" % i, [128, 256], BF16, ph) for i in range(2)]
            it = 0
            for T2 in range(NB * 8):
                xn_, xnk = xn2t[T2 % 2], "xn2t%d" % (T2 % 2)
                h_, hk = h1t[T2 % 2], "h1t%d" % (T2 % 2)
                P.dma(xn_[:], XN2_d[T2], reads=[("XN2d", T2)], writes=[xnk])
                for fc in range(32):
                    bank = it % 4
                    s_ = sq[it % 2]
                    sk = "sq%d" % (it % 2)
                    it += 1
                    for kc in range(8):
                        P.add("pe", OPC("matmul", psb[bank][:, 0:256], lhsT=w1b[:, kc, fc * 128:(fc + 1) * 128], rhs=xn_[:, kc, :],
                            start=(kc == 0), stop=(kc == 7)),
                            reads=[("w1b", fc // 4), xnk], writes=[pk[bank]], group=("h1", it))
                    P.add("act", OPC("activation", out=s_[:], in_=psb[bank][:, 0:256], func=AF.Square),
                          reads=[], writes=[pk[bank], sk])
                    P.add("dve", OPC("scalar_tensor_tensor", out=h_[:, fc, :], in0=psb[bank][:, 0:256], scalar=0.0, in1=s_[:], op0=ALU.is_gt, op1=ALU.mult),
                        reads=[sk], writes=[pk[bank], hk])
                P.dma(H1T_d[T2], h_[:], reads=[hk], writes=[("H1Td", T2)], eng="pool")
            P.barrier(dummy[:])

        with phase() as ph:
            h1t = [sb("h1t%d" % i, [128, 32, 256], BF16, ph) for i in range(2)]
            x1t = [sb("x1t%d" % i, [128, 2, D], F32, ph) for i in range(2)]
            tmp = [sb("tmp%d" % i, [128, 512], F32, ph) for i in range(2)]
            ot = [sb("ot%d" % i, [128, D], F32, ph) for i in range(2)]
            junk4 = sb("junk4", [128, D], F32, ph)
            st4 = sb("st4", [128, 4], F32, ph)
            it = 0
            for T2 in range(NB * 8):
                b = T2 // 8
                h_, hk = h1t[T2 % 2], "h1t%d" % (T2 % 2)
                x1_, x1k = x1t[T2 % 2], "x1t%d" % (T2 % 2)
                P.dma(h_[:], H1T_d[T2], reads=[("H1Td", T2)], writes=[hk])
                for j in range(2):
                    ch = T2 * 2 + j
                    P.dma(x1_[:, j, :], X1_d[ch], reads=[("X1d", ch)], writes=[(x1k, j)])
                for j in range(2):
                    ch = T2 * 2 + j
                    o_, ok = ot[ch % 2], "ot%d" % (ch % 2)
                    for nb_ in range(2):
                        bank = it % 4
                        it += 1
                        for fc in range(32):
                            P.add("pe", OPC("matmul", psb[bank][:, :], lhsT=h_[:, fc, j * 128:(j + 1) * 128],
                                rhs=w2b[:, fc, nb_ * 512:(nb_ + 1) * 512], start=(fc == 0), stop=(fc == 31)),
                                reads=[("w2b", i_) for i_ in range(8)] + [hk], writes=[pk[bank]], group=("o2", it))
                        tm = tmp[nb_]
                        tk = "tmp%d" % nb_
                        P.add("dve", OPC("tensor_tensor", out=tm[:], in0=psb[bank][:, :], in1=g2bc[:, b, nb_ * 512:(nb_ + 1) * 512], op=ALU.mult),
                            reads=["gbc"], writes=[pk[bank], tk])
                        P.add("pool", OPC("tensor_tensor", out=x1_[:, j, nb_ * 512:(nb_ + 1) * 512], in0=tm[:], in1=x1_[:, j, nb_ * 512:(nb_ + 1) * 512],
                            op=ALU.add), reads=[tk, (x1k, j)], writes=[(x1k, j)])
                    P.add("act", OPC("activation", out=junk4[:], in_=x1_[:, j, :], func=AF.Square,
                                                                       accum_out=st4[:, 0:1]),
                          reads=[(x1k, j)], writes=["junk4", "st4"])
                    P.add("act", OPC("activation", out=st4[:, 1:2], in_=st4[:, 0:1], func=AF.Sqrt,
                                                        scale=1.0 / D, bias=EPS), reads=["st4"], writes=["st4b"])
                    P.add("dve", OPC("reciprocal", out=st4[:, 2:3], in_=st4[:, 1:2]), reads=["st4b"], writes=["st4c"])
                    P.add("dve", OPC("scalar_tensor_tensor", out=o_[:], in0=x1_[:, j, :], scalar=st4[:, 2:3], in1=rowsb[:, R_NF:R_NF + D],
                        op0=ALU.mult, op1=ALU.mult), reads=[(x1k, j), "st4c", "rowsb"], writes=[ok])
                    tok0 = (ch % 16) * 128
                    P.dma(out_d[b, tok0:tok0 + 128, :], o_[:], reads=[ok], writes=[("outd", ch)], eng="pool")
        P.emit()
    return nc


def _host_inputs(x, c, ctx, c_ctx, norm1, norm2, w_mod, b_mod, w_in, conv_qk, b_gate, g_mh,
                 w_a, w_s, b_s, g_sgu, w_b, w_out, w1, w2, norm_f):
    f = np.float32

    def blocks(W):
        K, N = W.shape
        out = []
        for kg in range(K // 1024):
            for nb_ in range(N // 512):
                blk = W[kg * 1024:(kg + 1) * 1024, nb_ * 512:(nb_ + 1) * 512]
                out.append(blk.reshape(8, 128, 512).transpose(1, 0, 2))
        return out

    w_in0 = np.asarray(w_in[0], f)
    wl = []
    for n in ("q", "k", "v", "o", "u", "vg", "ga", "gb"):
        c0 = WIN_COLS[n]
        wl += blocks(w_in0[:, c0:c0 + 1024])
    wl += blocks(np.asarray(w_a[0], f)) + blocks(np.asarray(w_b[0], f)) + blocks(np.asarray(w_out[0], f))
    wl += blocks(np.asarray(w1[0], f))
    wl += blocks(np.asarray(w2[0], f))
    wall = np.ascontiguousarray(np.stack(wl, 0))
    assert wall.shape[0] == NWB
    wg = np.ascontiguousarray(w_in0[:, 3072:3088].reshape(8, 128, 16).transpose(1, 0, 2))
    wmod = np.ascontiguousarray(np.stack(blocks(np.asarray(w_mod[0], f)), 0))
    bm = np.asarray(b_mod[0], f)
    bmodc = np.ascontiguousarray(bm.reshape(48, 128).T)
    bmodr = np.ascontiguousarray(bm.reshape(1, -1))
    colsv = np.zeros((128, NCOLS), f)
    colsv[:, C_N1:C_N1 + 8] = np.asarray(norm1[0], f).reshape(8, 128).T
    colsv[:, C_N2:C_N2 + 8] = np.asarray(norm2[0], f).reshape(8, 128).T
    colsv[:, C_GMH:C_GMH + 8] = np.asarray(g_mh[0], f).reshape(8, 128).T
    cq = np.asarray(conv_qk[0], f)
    colsv[:, C_CONV:C_CONV + 48] = cq.reshape(3, 16, 128).transpose(2, 1, 0).reshape(128, 48)
    rowsv = np.concatenate([np.asarray(b_gate[0], f).reshape(-1), np.asarray(g_sgu[0], f).reshape(-1),
                            np.asarray(norm_f, f).reshape(-1), np.asarray(b_s[0], f).reshape(-1)]).reshape(1, -1)
    wsT = np.ascontiguousarray(np.asarray(w_s[0], f).transpose(2, 0, 1))
    eye = np.eye(128, dtype=f)
    triU = np.triu(np.ones((128, 128), f))
    triL = np.tril(np.ones((128, 128), f))
    consts = np.ascontiguousarray(np.stack([eye, triU, triL, np.ones((128, 128), f)], 1))
    maps = []
    xx = np.asarray(x, f)
    cc = np.asarray(c, f)
    cx = np.asarray(ctx, f)
    ccx = np.asarray(c_ctx, f)
    for i in range(8):
        sc = np.stack([cc[2 * i], cc[2 * i + 1], ccx], 1)
        scT = np.ascontiguousarray(sc.reshape(8, 128, 3).transpose(1, 0, 2))
        maps.append(dict(x=np.ascontiguousarray(xx[2 * i:2 * i + 2]), ctx=np.ascontiguousarray(cx[2 * i:2 * i + 2]),
                         scT=scT, wmod=wmod, bmodc=bmodc, bmodr=bmodr, wall=wall, wg=wg, cols=colsv,
                         rows=np.ascontiguousarray(rowsv), wsT=wsT, consts=consts))
    return maps


def kernel(**inputs):
    maps = _host_inputs(**inputs)
    nc = build_nc()
    res = run_bass_kernel_spmd(nc, maps, core_ids=list(range(8)))
    out = np.concatenate([np.asarray(r["out"]) for r in res.results], axis=0)
    return out.astype(np.float32)
```

```python
import numpy as np
import ml_dtypes
from contextlib import ExitStack
import concourse.bass as bass
import concourse.mybir as mybir
from concourse.bass_utils import run_bass_kernel_spmd

F32 = mybir.dt.float32
BF16 = mybir.dt.bfloat16
AF = mybir.ActivationFunctionType
ALU = mybir.AluOpType

D = 1024
S = 2048
CL = 256
NB = 2
SEQ = S + CL
NCH = SEQ // 128
EPS = 1e-6
DEBUG = False


class Op:
    __slots__ = ("eng", "fn", "deps", "group", "dma", "needs_inc", "sem", "val",
                 "prev_same_sem")

    def __init__(self, eng, fn, group, dma):
        self.eng = eng
        self.fn = fn
        self.deps = []
        self.group = group
        self.dma = dma
        self.needs_inc = False
        self.sem = None
        self.val = 0
        self.prev_same_sem = None


class Prog:
    ENGS = ("pe", "act", "dve", "pool", "sp")

    def __init__(self, nc, n_dma_sems=14):
        self.nc = nc
        self.ops = {e: [] for e in self.ENGS}
        self.last_w = {}
        self.readers = {}
        self.n_dma_sems = n_dma_sems
        self.pending_dmas = []
        self.bar = None
        self.bar_seen = {e: None for e in self.ENGS}

    def add(self, eng, fn, reads=(), writes=(), group=None, dma=False):
        op = Op(eng, fn, group, dma)
        deps = {}
        for k in reads:
            w = self.last_w.get(k)
            if w is not None:
                deps[id(w)] = w
        for k in writes:
            w = self.last_w.get(k)
            if w is not None and not (group is not None and w.group == group):
                deps[id(w)] = w
            rd = self.readers.get(k)
            if rd:
                for r in rd[0].values():
                    deps[id(r)] = r
                for r in rd[1]:
                    deps[id(r)] = r
        if self.bar is not None and self.bar_seen[eng] is not self.bar:
            deps[id(self.bar)] = self.bar
            self.bar_seen[eng] = self.bar
        for d in deps.values():
            if d is op:
                continue
            if (not d.dma) and (not dma) and d.eng == "pe" and eng == "pe":
                continue
            op.deps.append(d)
        for k in reads:
            rd = self.readers.get(k)
            if rd is None:
                rd = self.readers[k] = ({}, [])
            if dma:
                rd[1].append(op)
            else:
                rd[0][eng] = op
        for k in writes:
            self.last_w[k] = op
            self.readers[k] = None
        self.ops[eng].append(op)
        if dma:
            self.pending_dmas.append(op)
        return op

    def dma(self, out, in_, reads=(), writes=(), eng="sp"):
        return self.add(eng, OPC("dma_start", out=out, in_=in_), reads, writes, dma=True)

    def barrier(self, dummy):
        j = Op("dve", OPC("memset", dummy, 0.0), None, False)
        for e in self.ENGS:
            if e == "sp":
                continue
            for o in reversed(self.ops[e]):
                if not o.dma:
                    j.deps.append(o)
                    break
        j.deps.extend(self.pending_dmas)
        self.pending_dmas = []
        self.ops["dve"].append(j)
        self.bar = j
        self.bar_seen = {e: None for e in self.ENGS}
        self.last_w = {}
        self.readers = {}

    def emit(self):
        nc = self.nc
        for e in self.ENGS:
            for op in self.ops[e]:
                for d in op.deps:
                    d.needs_inc = True
        with ExitStack() as st:
            esem = {e: st.enter_context(nc.semaphore("s_" + e)) for e in self.ENGS}
            dsem = {e: [st.enter_context(nc.semaphore("d_%s_%d" % (e, i)))
                        for i in range(self.n_dma_sems)] for e in ("sp", "act", "pool")}
            for e in self.ENGS:
                cnt = 0
                ndma = 0
                dcnt = [0] * self.n_dma_sems
                lastd = [None] * self.n_dma_sems
                for op in self.ops[e]:
                    if op.dma:
                        i = ndma % self.n_dma_sems
                        ndma += 1
                        dcnt[i] += 16
                        op.sem = dsem[e][i]
                        op.val = dcnt[i]
                        op.prev_same_sem = lastd[i]
                        lastd[i] = op
                        op.needs_inc = True
                    elif op.needs_inc:
                        cnt += 1
                        op.sem = esem[e]
                        op.val = cnt
            block = st.enter_context(nc.Block())

            def run(ename, eng):
                seen = {}
                for op in self.ops[ename]:
                    waits = {}
                    ds = list(op.deps)
                    if op.dma and op.prev_same_sem is not None:
                        ds.append(op.prev_same_sem)
                    for d in ds:
                        key = id(d.sem)
                        if seen.get(key, 0) >= d.val:
                            continue
                        if key not in waits or waits[key][1] < d.val:
                            waits[key] = (d.sem, d.val)
                    for key, (s, v) in waits.items():
                        eng.wait_ge(s, v)
                        seen[key] = v
                    ins = op.fn(eng)
                    if op.needs_inc:
                        ins.then_inc(op.sem, 16 if op.dma else 1)
                lastd = {}
                for op in self.ops[ename]:
                    if op.dma:
                        lastd[id(op.sem)] = (op.sem, op.val)
                for s, v in lastd.values():
                    eng.wait_ge(s, v)

            @block.sync
            def _(eng):
                run("sp", eng)

            @block.scalar
            def _(eng):
                run("act", eng)

            @block.vector
            def _(eng):
                run("dve", eng)

            @block.gpsimd
            def _(eng):
                run("pool", eng)

            @block.tensor
            def _(eng):
                run("pe", eng)


def OPC(name, *args, **kwargs):
    return lambda e: getattr(e, name)(*args, **kwargs)


def bc(t, off, dims):
    ps = 1
    for s_ in t.shape[1:]:
        ps *= s_
    return bass.AP(t, off, [[ps, 128]] + [list(d) for d in dims])


WIN_COLS = dict(q=0, k=1024, v=2048, o=3088, u=4112, vg=5136, ga=6160, gb=7184)
WB = {}
_i = 0
for _n in ("q", "k", "v", "o", "u", "vg", "ga", "gb"):
    WB[_n] = _i
    _i += 2
WB["wa"] = 16
WB["wb"] = 18
WB["wout"] = 20
WB["w1"] = 22
WB["w2"] = 30
NWB = 38

C_N1, C_N2, C_GMH, C_CONV = 0, 8, 16, 24
NCOLS = 24 + 48
R_BG, R_GSGU, R_NF, R_BS = 0, 16, 16 + 1024, 16 + 2048
NROWS = 16 + 3072


def seq_tiles(L, T=384):
    out = []
    t = 0
    while t < L:
        n = min(T, L - t)
        out.append((t, n))
        t += n
    return out


def build_nc():
    nc = bass.Bass("TRN2", target_bir_lowering=False)
    dk = "ExternalOutput" if DEBUG else "Internal"

    def din(name, shape, dt=F32):
        return nc.dram_tensor(name, list(shape), dt, kind="ExternalInput").ap()

    def dscr(name, shape, dt):
        return nc.dram_tensor(name, list(shape), dt, kind=dk).ap()

    x_d = din("x", [NB, S, D])
    ctx_d = din("ctx", [NB, CL, D])
    scT_d = din("scT", [128, 8, 3])
    wmod_d = din("wmod", [12, 128, 8, 512])
    bmodc_d = din("bmodc", [128, 48])
    bmodr_d = din("bmodr", [1, 6 * D])
    wall_d = din("wall", [NWB, 128, 8, 512])
    wg_d = din("wg", [128, 8, 16])
    cols_d = din("cols", [128, NCOLS])
    rows_d = din("rows", [1, NROWS])
    wsT_d = din("wsT", [128, 8, 128])
    consts_d = din("consts", [128, 4, 128])
    out_d = nc.dram_tensor("out", [NB, S, D], F32, kind="ExternalOutput").ap()

    KT_d = dscr("KT", [NB * NCH, 128, 8, 128], BF16)
    QT_d = dscr("QT", [NB * NCH, 128, 8, 128], BF16)
    VP_d = dscr("VP", [NB * NCH, 128, 2 * 4 * 257], BF16)
    SO_d = dscr("SO", [NB * 16, 128, D], BF16)
    UT_d = dscr("UT", [NB * 4, 128, 8, 512], BF16)
    VN_d = dscr("VN", [NB * 16, 128, D], BF16)
    SGA_d = dscr("SGA", [NB * 4, 128, 8, 512], BF16)
    SGB_d = dscr("SGB", [NB * 4, 128, 8, 512], BF16)
    CB_d = dscr("CB", [NB * 16, 128, 4 * 2 * 257], BF16)
    HMT_d = dscr("HMT", [NB * 16, 128, 8, 128], BF16)
    X1_d = dscr("X1", [NB * 16, 128, D], F32)
    XN2_d = dscr("XN2", [NB * 8, 128, 8, 256], BF16)
    H1T_d = dscr("H1T", [NB * 8, 128, 32, 256], BF16)

    P = Prog(nc)
    es = ExitStack()
    alloc = {"top": 16512, "n": 0}
    stack = []

    class phase:
        def __enter__(self):
            stack.append(alloc["top"])
            return self

        def __exit__(self, *a):
            alloc["top"] = stack.pop()
            return False

    with es:
        def sb(name, shape, dt, stk=None):
            nb_ = 2 if dt == BF16 else 4
            for s_ in shape[1:]:
                nb_ *= s_
            nb_ = (nb_ + 63) // 64 * 64
            off = alloc["top"]
            alloc["top"] += nb_
            assert alloc["top"] <= 229344, ("SBUF overflow", name, alloc["top"])
            alloc["n"] += 1
            return nc.alloc_sbuf_tensor_at("%s_%d" % (name, alloc["n"]), list(shape), dt, offset=off)

        psb = [es.enter_context(nc.psum_tensor("ps%d" % i, [128, 512], F32)) for i in range(8)]
        pk = ["ps%d" % i for i in range(8)]

        consts = sb("consts", [128, 4, 128], F32)
        identb = sb("identb", [128, 128], BF16)
        mF = sb("mF", [128, 128], F32)
        mB = sb("mB", [128, 128], F32)
        cols = sb("cols", [128, NCOLS], F32)
        bmodc = sb("bmodc", [128, 48], F32)
        modc = sb("modc", [128, 48, 3], F32)
        gcol1 = sb("gcol1", [128, 8, 3], F32)
        gcol2 = sb("gcol2", [128, 8, 3], F32)
        g1bc = sb("g1bc", [128, NB, D], F32)
        g2bc = sb("g2bc", [128, NB, D], F32)
        rowsb = sb("rowsb", [128, NROWS], F32)
        wsT = sb("wsTb", [128, 8, 128], BF16)
        wgb = sb("wgb", [128, 8, 16], BF16)
        GATES = sb("GATES", [128, NB * NCH, 24], F32)
        dummy = sb("dummy", [128, 1], F32)
        GS = NB * NCH * 24

        ident = consts[:, 0, :]
        triU = consts[:, 1, :]
        triL = consts[:, 2, :]
        ones = consts[:, 3, :]

        P.dma(consts[:], consts_d, writes=["consts"])
        P.dma(cols[:], cols_d, writes=["cols"])
        P.dma(bmodc[:], bmodc_d, writes=["bmodc"])
        P.dma(rowsb[:], bass.AP(rows_d.tensor, 0, [[0, 128], [1, NROWS]]), writes=["rowsb"])
        P.add("dve", OPC("tensor_copy", out=identb[:], in_=ident), reads=["consts"], writes=["identb"])
        P.add("dve", OPC("tensor_scalar", out=mF[:], in0=triU, scalar1=0.0625, scalar2=None, op0=ALU.mult),
              reads=["consts"], writes=["mF"])
        P.add("dve", OPC("tensor_scalar", out=mB[:], in0=triL, scalar1=0.0625, scalar2=None, op0=ALU.mult),
              reads=["consts"], writes=["mB"])

        with phase() as ph:
            scT = sb("scT", [128, 8, 3], F32, ph)
            screp = sb("screp", [128, NB, 8, 128], F32, ph)
            wst = [sb("wst%d" % i, [128, 8, 512], F32, ph) for i in range(2)]
            wstmp = sb("wstmp", [128, 8, 128], F32, ph)
            wgst = sb("wgst", [128, 8, 16], F32, ph)
            bmr = sb("bmr", [128, 2 * D], F32, ph)
            modrow = sb("modrow", [3, 4 * D], F32, ph)
            bmr3 = sb("bmr3", [3, 4 * D], F32, ph)
            for mm, m in enumerate((0, 1, 3, 4)):
                P.dma(bmr3[0:3, mm * D:(mm + 1) * D], bass.AP(bmodr_d.tensor, m * D, [[0, 3], [1, D]]), writes=[("bmr3", mm)])
            P.dma(bmr[:, 0:D], bass.AP(bmodr_d.tensor, 2 * D, [[0, 128], [1, D]]), writes=["bmr"])
            P.dma(bmr[:, D:2 * D], bass.AP(bmodr_d.tensor, 5 * D, [[0, 128], [1, D]]), writes=["bmr2"])
            P.dma(scT[:], scT_d, writes=["scT"])
            P.dma(wstmp[:], wsT_d, writes=["wstmp"])
            P.dma(wgst[:], wg_d, writes=["wgst"])
            P.add("pool", OPC("tensor_copy", out=wsT[:], in_=wstmp[:]), reads=["wstmp"], writes=["wsT"])
            P.add("pool", OPC("tensor_copy", out=wgb[:], in_=wgst[:]), reads=["wgst"], writes=["wgb"])
            P.add("act", OPC("activation", out=scT[:], in_=scT[:], func=AF.Silu), reads=["scT"], writes=["scT"])
            for b in range(NB):
                P.add("dve", OPC("tensor_copy", out=screp[:, b, :, :], in_=bc(scT, b, [[3, 8], [0, 128]])), reads=["scT"], writes=["screp"])
            for j in range(12):
                m = j // 2
                w = wst[j % 2]
                wk = "wst%d" % (j % 2)
                P.dma(w[:], wmod_d[j], writes=[wk])
                if m in (2, 5):
                    dst = g1bc if m == 2 else g2bc
                    for b in range(NB):
                        bank = (2 * j + b) % 4
                        for kc in range(8):
                            P.add("pe", OPC("matmul", psb[bank][:, :], lhsT=screp[:, b, kc, :], rhs=w[:, kc, :],
                                start=(kc == 0), stop=(kc == 7)),
                                reads=[wk, "screp"], writes=[pk[bank]], group=("m0", j, b))
                        c0 = (j % 2) * 512
                        r0 = (0 if m == 2 else D) + c0
                        P.add("dve", OPC("tensor_tensor", out=dst[:, b, c0:c0 + 512], in0=psb[bank][:, :],
                            in1=bmr[:, r0:r0 + 512], op=ALU.add),
                            reads=["bmr", "bmr2"], writes=[pk[bank], "gbc"])
                else:
                    bank = 4 + (j % 2)
                    mm = {0: 0, 1: 1, 3: 2, 4: 3}[m]
                    for kc in range(8):
                        P.add("pe", OPC("matmul", psb[bank][0:3, :], lhsT=scT[:, kc, :], rhs=w[:, kc, :],
                                        start=(kc == 0), stop=(kc == 7)),
                              reads=[wk, "scT"], writes=[pk[bank]], group=("m0r", j))
                    c0 = mm * D + (j % 2) * 512
                    P.add("dve", OPC("tensor_tensor", out=modrow[0:3, c0:c0 + 512], in0=psb[bank][0:3, :],
                                     in1=bmr3[0:3, c0:c0 + 512], op=ALU.add),
                          reads=[("bmr3", mm)], writes=[pk[bank], ("modrow", mm)])
            for mm, m in enumerate((0, 1, 3, 4)):
                for kc in range(8):
                    P.add("pe", OPC("transpose", out=psb[6][:, (mm * 8 + kc) * 3:(mm * 8 + kc) * 3 + 3],
                                    in_=modrow[0:3, mm * D + kc * 128:mm * D + (kc + 1) * 128], identity=consts[0:3, 0, 0:3]),
                          reads=[("modrow", mm), "consts"], writes=[pk[6]], group=("m0t", mm))
                P.add("dve", OPC("tensor_copy", out=modc[:, m * 8:(m + 1) * 8, :],
                                 in_=psb[6][:, mm * 24:(mm + 1) * 24].rearrange("p (k c) -> p k c", k=8)),
                      reads=[], writes=[pk[6], "modc"])
            for (gc, mi, cn) in ((gcol1, 1, C_N1), (gcol2, 4, C_N2)):
                P.add("dve", OPC("tensor_scalar", out=gc[:], in0=modc[:, mi * 8:(mi + 1) * 8, :], scalar1=1.0, scalar2=None, op0=ALU.add),
                    reads=["modc"], writes=["gcol"])
                P.add("dve", OPC("tensor_tensor", out=gc[:], in0=gc[:], in1=bc(cols, cn, [[1, 8], [0, 3]]), op=ALU.mult),
                    reads=["gcol", "cols"], writes=["gcol"])
            P.barrier(dummy[:])

        for b in range(NB):
            with phase() as ph:
                xnT = sb("xnT", [128, 8, SEQ], BF16, ph)
                xin = [sb("xin%d" % i, [128, D], F32, ph) for i in range(2)]
                xsb = [sb("xsb%d" % i, [128, D], BF16, ph) for i in range(2)]
                junk = sb("junk", [128, D], F32, ph)
                st1 = sb("st1", [128, 8], F32, ph)
                st2 = sb("st2", [128, 8], F32, ph)
                wbf = [sb("wbf%d" % i, [128, 8, 1024], BF16, ph) for i in range(3)]
                Gs = sb("Gs", [128, 16], F32, ph)
                sp_ = sb("sp_", [128, 2, 4], F32, ph)
                uu = sb("uu", [128, 2, 4], F32, ph)
                VPt = [sb("VPt%d" % i, [128, 2, 4, 257], BF16, ph) for i in range(2)]
                acc = [sb("acc%d" % i, [128, 384], F32, ph) for i in range(4)]
                ob = [sb("ob%d" % i, [128, 8, 512], BF16, ph) for i in range(2)]
                sot = [sb("sot%d" % i, [128, D], BF16, ph) for i in range(2)]
                gv = [sb("gv%d" % i, [128, D], F32, ph) for i in range(2)]
                vn = [sb("vn%d" % i, [128, D], BF16, ph) for i in range(2)]
                groups = ["v", "k", "q", "o", "u", "vg", "ga", "gb"]

                def load_w(gidx):
                    for hb in range(2):
                        load_blk(gidx, hb)

                def load_blk(gidx, hb):
                    name = groups[gidx]
                    wi = gidx % 3
                    P.dma(wbf[wi][:, :, hb * 512:(hb + 1) * 512], wall_d[WB[name] + hb], writes=[("wbf", wi, hb)], eng="pool")

                def wk(wi):
                    return [("wbf", wi, 0), ("wbf", wi, 1)]

                def ln1_stats(c):
                    xi = xin[c % 2]
                    xk = "xin%d" % (c % 2)
                    o4 = 4 * (c % 2)
                    sk_ = "st1_%d" % (c % 2)
                    src = x_d[b, c * 128:(c + 1) * 128, :] if c < 16 else ctx_d[b, (c - 16) * 128:(c - 15) * 128, :]
                    P.dma(xi[:], src, writes=[xk])
                    P.add("act", OPC("activation", out=junk[:], in_=xi[:], func=AF.Square, accum_out=st1[:, o4:o4 + 1]),
                          reads=[xk], writes=["junk", sk_])
                    P.add("act", OPC("activation", out=st1[:, o4 + 1:o4 + 2], in_=st1[:, o4:o4 + 1], func=AF.Sqrt,
                                     scale=1.0 / D, bias=EPS), reads=[sk_], writes=[sk_ + "b"])
                    P.add("dve", OPC("reciprocal", out=st1[:, o4 + 2:o4 + 3], in_=st1[:, o4 + 1:o4 + 2]),
                          reads=[sk_ + "b"], writes=[sk_ + "c"])
                    P.add("dve", OPC("tensor_scalar", out=xsb[c % 2][:], in0=xi[:], scalar1=st1[:, o4 + 2:o4 + 3], scalar2=None,
                                     op0=ALU.mult), reads=[xk, sk_ + "c"], writes=["xsb%d" % (c % 2)])

                def ln1_tr(c):
                    mcol = b if c < 16 else 2
                    xs = xsb[c % 2]
                    xsk = "xsb%d" % (c % 2)
                    bks = (2 * (c % 2), 2 * (c % 2) + 1)
                    pTs = [psb[bk][:, :].bitcast(BF16) for bk in bks]
                    for kc in range(8):
                        hf = kc % 2
                        P.add("pe", OPC("transpose", out=pTs[hf][:, (kc // 2) * 128:(kc // 2 + 1) * 128],
                                        in_=xs[:, kc * 128:(kc + 1) * 128], identity=identb[:]),
                              reads=[xsk, "identb"], writes=[pk[bks[hf]]], group=("tr1", b, c, hf))
                    for kc in range(0, 8, 2):
                        P.add("act", OPC("activation", out=xnT[:, kc, c * 128:(c + 1) * 128],
                                         in_=pTs[0][:, (kc // 2) * 128:(kc // 2 + 1) * 128], func=AF.Identity,
                                         scale=gcol1[:, kc, mcol:mcol + 1], bias=modc[:, kc, mcol:mcol + 1]),
                              reads=[], writes=[pk[bks[0]], ("xnTa", c)], group=("ev1a", b, c))
                    for kc in range(1, 8, 2):
                        P.add("dve", OPC("tensor_scalar", out=xnT[:, kc, c * 128:(c + 1) * 128],
                                         in0=pTs[1][:, (kc // 2) * 128:(kc // 2 + 1) * 128],
                                         scalar1=gcol1[:, kc, mcol:mcol + 1], scalar2=modc[:, kc, mcol:mcol + 1],
                                         op0=ALU.mult, op1=ALU.add),
                              reads=[], writes=[pk[bks[1]], ("xnTb", c)], group=("ev1b", b, c))

                wl = {2: (0, 0), 5: (0, 1), 8: (1, 0), 11: (1, 1)}
                ln1_stats(0)
                for c in range(NCH):
                    if c + 1 < NCH:
                        ln1_stats(c + 1)
                    ln1_tr(c)
                    if c in wl:
                        load_blk(*wl[c])

                def xr(c):
                    return [("xnTa", c), ("xnTb", c)]
                xr_all = [k_ for c in range(NCH) for k_ in xr(c)]
                def gate_a(c):
                    gi = b * NCH + c
                    bank = 4 + (c % 2)
                    for kc in range(8):
                        P.add("pe", OPC("matmul", psb[bank][:, 0:16], lhsT=xnT[:, kc, c * 128:(c + 1) * 128], rhs=wgb[:, kc, :],
                                        start=(kc == 0), stop=(kc == 7)),
                              reads=xr(c) + ["wgb"], writes=[pk[bank]], group=("g", b, c))
                    P.add("dve", OPC("tensor_tensor", out=Gs[:], in0=psb[bank][:, 0:16], in1=rowsb[:, R_BG:R_BG + 16], op=ALU.add),
                          reads=["rowsb"], writes=[pk[bank], "Gs"])
                    P.add("act", OPC("activation", out=sp_[:], in_=bc(Gs, 4, [[8, 2], [1, 4]]), func=AF.Exp, scale=-1.0),
                          reads=["Gs"], writes=["sp_"])
                    P.add("act", OPC("activation", out=sp_[:], in_=sp_[:], func=AF.Ln, bias=1.0), reads=["sp_"], writes=["sp_"])

                def gate_b(c):
                    gi = b * NCH + c
                    bank2 = 6 + (c % 2)
                    P.add("pe", OPC("matmul", psb[bank2][:, 0:4], lhsT=triU, rhs=sp_[:, 0, :], start=True, stop=True),
                          reads=["sp_", "consts"], writes=[pk[bank2]], group=("cs", b, c))
                    P.add("pe", OPC("matmul", psb[bank2][:, 4:8], lhsT=triL, rhs=sp_[:, 1, :], start=True, stop=True),
                          reads=["sp_", "consts"], writes=[pk[bank2]], group=("cs", b, c))
                    P.add("pe", OPC("matmul", psb[bank2][:, 8:16], lhsT=ones, rhs=bc(sp_, 0, [[1, 8]]), start=True, stop=True),
                          reads=["sp_", "consts"], writes=[pk[bank2]], group=("cs", b, c))
                    P.add("dve", OPC("tensor_tensor", out=uu[:], in0=bc(psb[bank2], 0, [[4, 2], [1, 4]]),
                                     in1=bc(Gs, 0, [[8, 2], [1, 4]]), op=ALU.add), reads=["Gs"], writes=[pk[bank2], "uu"])
                    P.add("act", OPC("activation", out=GATES[:, gi, 0:8], in_=bc(uu, 0, [[1, 8]]), func=AF.Exp),
                          reads=["uu"], writes=[("G", gi)], group=("gw", gi))
                    P.add("act", OPC("activation", out=GATES[:, gi, 8:16], in_=psb[bank2][:, 0:8], func=AF.Exp),
                          reads=[], writes=[pk[bank2], ("G", gi)], group=("gw", gi))
                    P.add("act", OPC("activation", out=GATES[:, gi, 16:24], in_=psb[bank2][:, 8:16], func=AF.Exp, scale=-1.0),
                          reads=[], writes=[pk[bank2], ("G", gi)], group=("gw", gi))
                load_w(2)
                gate_a(0)
                gate_b(0)
                for c in range(NCH):
                    if c + 1 < NCH:
                        gate_a(c + 1)
                    gi = b * NCH + c
                    vp = VPt[c % 2]
                    vk = "VPt%d" % (c % 2)
                    for hb in range(2):
                        bank = (2 * c + hb) % 4
                        for kc in range(8):
                            P.add("pe", OPC("matmul", psb[bank][:, :], lhsT=xnT[:, kc, c * 128:(c + 1) * 128],
                                            rhs=wbf[0][:, kc, hb * 512:(hb + 1) * 512], start=(kc == 0), stop=(kc == 7)),
                                  reads=xr(c) + [("wbf", 0, hb)], writes=[pk[bank]], group=("v", b, c, hb))
                        for d_ in range(2):
                            P.add("dve", OPC("tensor_tensor", out=vp[:, d_, 2 * hb:2 * hb + 2, 0:256],
                                             in0=bc(psb[bank], 0, [[256, 2], [1, 256]]),
                                             in1=bc(GATES, gi * 24 + d_ * 4 + 2 * hb, [[1, 2], [0, 256]]), op=ALU.mult),
                                  reads=[("G", gi)], writes=[pk[bank], vk], group=("vev", gi, hb))
                    P.add("act", OPC("activation", out=vp[:, :, :, 256], in_=bc(GATES, gi * 24, [[4, 2], [1, 4]]), func=AF.Copy),
                          reads=[("G", gi)], writes=[vk])
                    P.dma(VP_d[gi], vp[:].rearrange("p a h v -> p (a h v)"), reads=[vk], writes=[("VPd", gi)], eng="pool")
                    if c + 1 < NCH:
                        gate_b(c + 1)
                load_w(3)

                def qk_phase(name, wi, dst_d, cbase):
                    it = 0
                    ti = 0
                    pend = []
                    for (seq0, L) in ((0, S), (S, CL)):
                        for (t0, n) in seq_tiles(L):
                            a = max(t0 - 1, 0)
                            bnd = min(t0 + n + 1, L)
                            nw = bnd - a
                            c0 = t0 - a
                            lo = 1 if t0 == 0 else 0
                            hi = n - 1 if t0 + n == L else n
                            q_ = ob[ti % 2]
                            qk_ = "ob%d" % (ti % 2)
                            ti += 1
                            for cp in range(4):
                                ccs = (2 * cp, 2 * cp + 1)
                                bks = [(it + i) % 8 for i in range(2)]
                                acs = [acc[(it + i) % 4] for i in range(2)]
                                aks = ["acc%d" % ((it + i) % 4) for i in range(2)]
                                it += 2
                                for i, cc in enumerate(ccs):
                                    for kc in range(8):
                                        P.add("pe", OPC("matmul", psb[bks[i]][:, 0:nw], lhsT=wbf[wi][:, kc, cc * 128:(cc + 1) * 128],
                                                        rhs=xnT[:, kc, seq0 + a:seq0 + a + nw], start=(kc == 0), stop=(kc == 7)),
                                              reads=xr_all + [("wbf", wi, cc // 4)], writes=[pk[bks[i]]], group=("qk", name, it, i))
                                cws = [C_CONV + (cbase + cc) * 3 for cc in ccs]
                                for i in range(2):
                                    P.add("act", OPC("activation", out=acs[i][:, 0:n], in_=psb[bks[i]][:, c0:c0 + n], func=AF.Copy,
                                                     scale=cols[:, cws[i] + 1:cws[i] + 2]),
                                          reads=["cols"], writes=[pk[bks[i]], aks[i]])
                                for i in range(2):
                                    P.add("dve", OPC("scalar_tensor_tensor", out=acs[i][:, lo:n], in0=psb[bks[i]][:, c0 - 1 + lo:c0 - 1 + n],
                                                     scalar=cols[:, cws[i]:cws[i] + 1], in1=acs[i][:, lo:n], op0=ALU.mult, op1=ALU.add),
                                          reads=[aks[i], "cols"], writes=[pk[bks[i]], aks[i]])
                                for i in range(2):
                                    P.add("dve", OPC("scalar_tensor_tensor", out=acs[i][:, 0:hi], in0=psb[bks[i]][:, c0 + 1:c0 + 1 + hi],
                                                     scalar=cols[:, cws[i] + 2:cws[i] + 3], in1=acs[i][:, 0:hi], op0=ALU.mult, op1=ALU.add),
                                          reads=[aks[i], "cols"], writes=[pk[bks[i]], aks[i]])
                                for fn_ in pend:
                                    fn_()
                                del pend[:]
                                for i, cc in enumerate(ccs):
                                    pend.append(lambda i=i, cc=cc, acs=acs, aks=aks, q_=q_, qk_=qk_, n=n, ti=ti: P.add(
                                        "act", OPC("activation", out=q_[:, cc, 0:n], in_=acs[i][:, 0:n], func=AF.Silu),
                                        reads=[aks[i]], writes=[qk_], group=("qo", name, ti)))
                            for fn_ in pend:
                                fn_()
                            del pend[:]
                            for jj in range(n // 128):
                                ch = b * NCH + (seq0 + t0) // 128 + jj
                                P.dma(dst_d[ch], q_[:, :, jj * 128:(jj + 1) * 128], reads=[qk_], writes=[(name, ch)], eng="pool")

                qk_phase("KTd", 1, KT_d, 8)
                load_w(4)
                qk_phase("QTd", 2, QT_d, 0)
                load_w(5)

                for c in range(16):
                    so = sot[c % 2]
                    sk = "sot%d" % (c % 2)
                    for hb in range(2):
                        bank = (2 * c + hb) % 4
                        for kc in range(8):
                            P.add("pe", OPC("matmul", psb[bank][:, :], lhsT=xnT[:, kc, c * 128:(c + 1) * 128],
                                            rhs=wbf[0][:, kc, hb * 512:(hb + 1) * 512], start=(kc == 0), stop=(kc == 7)),
                                  reads=xr(c) + [("wbf", 0, hb)], writes=[pk[bank]], group=("o", b, c, hb))
                        P.add("act", OPC("activation", out=so[:, hb * 512:(hb + 1) * 512], in_=psb[bank][:, :], func=AF.Sigmoid),
                              reads=[], writes=[pk[bank], sk], group=("so", b, c))
                    P.dma(SO_d[b * 16 + c], so[:], reads=[sk], writes=[("SOd", b * 16 + c)], eng="pool")
                load_w(6)

                fmc = [0]

                def fm_phase(name, wi, func, dst_d):
                    it = 4
                    for t in range(4):
                        f_ = ob[fmc[0] % 2]
                        fk = "ob%d" % (fmc[0] % 2)
                        fmc[0] += 1
                        for cc in range(8):
                            bank = 4 + it % 4
                            it += 1
                            for kc in range(8):
                                P.add("pe", OPC("matmul", psb[bank][:, :], lhsT=wbf[wi][:, kc, cc * 128:(cc + 1) * 128],
                                                rhs=xnT[:, kc, t * 512:(t + 1) * 512], start=(kc == 0), stop=(kc == 7)),
                                      reads=xr_all + [("wbf", wi, cc // 4)], writes=[pk[bank]], group=("fm", name, t, cc))
                            P.add("act", OPC("activation", out=f_[:, cc, :], in_=psb[bank][:, :], func=func),
                                  reads=[], writes=[pk[bank], fk], group=("ft", name, t))
                        P.dma(dst_d[b * 4 + t], f_[:], reads=[fk], writes=[(name, b * 4 + t)], eng="pool")

                fm_phase("UTd", 1, AF.Gelu_apprx_tanh, UT_d)
                load_w(7)
                def vg_a(c):
                    g_ = gv[c % 2]
                    gk = "gv%d" % (c % 2)
                    o4 = 0 if c % 2 == 0 else 3
                    sk_ = "st2_%d" % (c % 2)
                    for hb in range(2):
                        bank = (2 * c + hb) % 4
                        for kc in range(8):
                            P.add("pe", OPC("matmul", psb[bank][:, :], lhsT=xnT[:, kc, c * 128:(c + 1) * 128],
                                            rhs=wbf[2][:, kc, hb * 512:(hb + 1) * 512], start=(kc == 0), stop=(kc == 7)),
                                  reads=xr(c) + [("wbf", 2, hb)], writes=[pk[bank]], group=("vg", b, c, hb))
                        P.add("act", OPC("activation", out=g_[:, hb * 512:(hb + 1) * 512], in_=psb[bank][:, :], func=AF.Gelu_apprx_tanh),
                              reads=[], writes=[pk[bank], gk], group=("gv", b, c))
                    if c % 3 == 2:
                        P.add("act", OPC("activation", out=junk[:], in_=g_[:], func=AF.Square, accum_out=st2[:, o4:o4 + 1]),
                              reads=[gk], writes=["junk", sk_])
                    else:
                        P.add("dve", OPC("tensor_tensor", out=junk[:], in0=g_[:], in1=g_[:], op=ALU.mult),
                              reads=[gk], writes=["junk"])
                        P.add("dve", OPC("tensor_reduce", out=st2[:, o4:o4 + 1], in_=junk[:], axis=mybir.AxisListType.X, op=ALU.add),
                              reads=["junk"], writes=[sk_])

                def vg_b(c):
                    g_ = gv[c % 2]
                    gk = "gv%d" % (c % 2)
                    v_ = vn[c % 2]
                    vk = "vn%d" % (c % 2)
                    o4 = 0 if c % 2 == 0 else 3
                    sk_ = "st2_%d" % (c % 2)
                    P.add("act", OPC("activation", out=st2[:, o4 + 1:o4 + 2], in_=st2[:, o4:o4 + 1], func=AF.Sqrt, scale=1.0 / D, bias=EPS),
                          reads=[sk_], writes=[sk_ + "b"])
                    P.add("dve", OPC("reciprocal", out=st2[:, o4 + 2:o4 + 3], in_=st2[:, o4 + 1:o4 + 2]), reads=[sk_ + "b"], writes=[sk_ + "c"])
                    P.add("dve", OPC("scalar_tensor_tensor", out=v_[:], in0=g_[:], scalar=st2[:, o4 + 2:o4 + 3], in1=rowsb[:, R_GSGU:R_GSGU + D],
                                     op0=ALU.mult, op1=ALU.mult), reads=[gk, sk_ + "c", "rowsb"], writes=[vk])
                    P.dma(VN_d[b * 16 + c], v_[:], reads=[vk], writes=[("VNd", b * 16 + c)], eng="pool")

                vg_a(0)
                for c in range(16):
                    if c + 1 < 16:
                        vg_a(c + 1)
                    vg_b(c)
                fm_phase("SGAd", 0, AF.Sigmoid, SGA_d)
                fm_phase("SGBd", 1, AF.Sigmoid, SGB_d)
                P.barrier(dummy[:])

        ph_outer = phase()
        ph_outer.__enter__()
        wres = [sb("wres%d" % i, [128, 8, 1024], BF16) for i in range(3)]
        with phase() as ph:
            SW = []
            for b in range(NB):
                d = dict(
                    kTc=[sb("kTc%d" % i, [128, 8, 128], BF16, ph) for i in range(2)],
                    qTc=[sb("qTc%d" % i, [128, 8, 128], BF16, ph) for i in range(2)],
                    VPc=[sb("VPc%d" % i, [128, 2, 4, 257], BF16, ph) for i in range(2)],
                    CBc=[sb("CBc%d" % i, [128, 4, 2, 257], BF16, ph) for i in range(2)],
                    SOc=[sb("SOc%d" % i, [128, D], BF16, ph) for i in range(2)],
                    Ktok=[sb("Ktok%d" % i, [128, D], BF16, ph) for i in range(2)],
                    Z=sb("Z", [128, 4, 2, 257], F32, ph),
                    Cf=[sb("Cf%d" % i, [128, 4, 2, 257], BF16, ph) for i in range(2)],
                    aT=sb("aT", [128, 2, 4, 128], BF16, ph),
                    hh=sb("hh", [128, D], F32, ph),
                    hm=sb("hm", [128, D], BF16, ph),
                    hmT=[sb("hmT%d" % i, [128, 8, 128], BF16, ph) for i in range(1)],
                    sm=sb("sm", [128, 16], F32, ph),
                    junk2=sb("junk2", [128, 256], F32, ph),
                    cnt=0, prev=None,
                    banks=((0, 2, 4, 5) if b == 0 else (1, 3, 6, 7)),
                )
                SW.append(d)

            def chunk_step(b, c, d_, first, last, mode):
                w = SW[b]
                K = lambda s_: (s_, b)
                bA, bB, bC, bD = w["banks"]
                i2 = w["cnt"] % 2
                w["cnt"] += 1
                gi = b * NCH + c
                lat = c < 16
                Z, aT, hh, hm, sm, junk2 = w["Z"], w["aT"], w["hh"], w["hm"], w["sm"], w["junk2"]
                zks = [K(("Z", h, hf)) for h in range(4) for hf in range(2)]
                kt, ktk = w["kTc"][i2], K("kTc%d" % i2)
                vp, vpk = w["VPc"][i2], K("VPc%d" % i2)
                P.dma(kt[:], KT_d[gi], reads=[("KTd", gi)], writes=[ktk])
                if mode == "full":
                    P.dma(vp[:].rearrange("p a h v -> p (a h v)"), VP_d[gi], reads=[("VPd", gi)], writes=[vpk])
                else:
                    P.dma(vp[:, d_].rearrange("p h v -> p (h v)"), VP_d[gi][:, d_ * 1028:(d_ + 1) * 1028],
                          reads=[("VPd", gi)], writes=[vpk])
                if mode == "full":
                    qt, qtk = w["qTc"][i2], K("qTc%d" % i2)
                    cb, cbk = w["CBc"][i2], K("CBc%d" % i2)
                    so, sok = w["SOc"][i2], K("SOc%d" % i2)
                    P.dma(qt[:], QT_d[gi], reads=[("QTd", gi)], writes=[qtk])
                    P.dma(cb[:].rearrange("p h a v -> p (h a v)"), CB_d[b * 16 + c], reads=[("CBd", b * 16 + c)], writes=[cbk])
                    P.dma(so[:], SO_d[b * 16 + c], reads=[("SOd", b * 16 + c)], writes=[sok])
                ktok, ktokk = w["Ktok"][i2], K("Ktok%d" % i2)
                yield
                if not last:
                    pT = psb[bA][:, :].bitcast(BF16)
                    for cc in range(8):
                        P.add("pe", OPC("transpose", out=pT[:, cc * 128:(cc + 1) * 128], in_=kt[:, cc, :], identity=identb[:]),
                              reads=[ktk, "identb"], writes=[pk[bA]], group=("trk", gi, d_))
                    P.add("act", OPC("activation", out=ktok[:], in_=pT, func=AF.Copy, scale=0.0625),
                          reads=[], writes=[pk[bA], ktokk])
                need_c = (mode == "full") or (d_ == 1 and lat)
                Cf, cfk = w["Cf"][i2], K("Cf%d" % i2)
                gp = w["prev"]
                if need_c:
                    if first:
                        P.add("pool", OPC("memset", Cf[:], 0.0), writes=[cfk])
                    else:
                        for h in (0, 1, 2, 3):
                            P.add("act", OPC("activation", out=Cf[:, h].rearrange("p a v -> p (a v)"),
                                             in_=Z[:, h].rearrange("p a v -> p (a v)"), func=AF.Copy,
                                             scale=GATES[:, gp, 16 + d_ * 4 + h:17 + d_ * 4 + h]),
                                  reads=zks + [("G", gp)], writes=[cfk])
                    if d_ == 1:
                        P.dma(CB_d[b * 16 + c], Cf[:].rearrange("p h a v -> p (h a v)"), reads=[cfk],
                              writes=[("CBd", b * 16 + c)], eng="pool")
                yield
                if mode == "full":
                    for h in range(4):
                        for hf in range(2):
                            P.add("pe", OPC("matmul", psb[bB][:, h * 128:(h + 1) * 128], lhsT=kt[:, 2 * h + hf, :],
                                            rhs=qt[:, 2 * h + hf, :], start=(hf == 0), stop=(hf == 1)),
                                  reads=[ktk, qtk], writes=[pk[bB]], group=("S", gi, h))
                    for dd, msk in ((0, mF), (1, mB)):
                        P.add("dve", OPC("tensor_tensor", out=aT[:, dd, :, :], in0=bc(psb[bB], 0, [[128, 4], [1, 128]]),
                                         in1=bc(msk, 0, [[0, 4], [1, 128]]), op=ALU.mult),
                              reads=["mF", "mB"], writes=[pk[bB], K("aT")], group=("aT", gi))
                    yield
                    for h in range(4):
                        for dd in range(2):
                            cst = Cf if dd == 0 else cb
                            cstk = cfk if dd == 0 else cbk
                            col = dd * 4 + h
                            P.add("pe", OPC("matmul", psb[bB][:, col:col + 1], lhsT=aT[:, dd, h, :], rhs=vp[:, dd, h, 256:257],
                                            start=True, stop=False),
                                  reads=[K("aT"), vpk], writes=[pk[bB]], group=("den", gi, h, dd))
                            for hf in range(2):
                                P.add("pe", OPC("matmul", psb[bB][:, col:col + 1], lhsT=qt[:, 2 * h + hf, :], rhs=cst[:, h, hf, 256:257],
                                                start=False, stop=(hf == 1)),
                                      reads=[qtk, cstk], writes=[pk[bB]], group=("den", gi, h, dd))
                    smk_all = [K(("sm", h, dd)) for h in range(4) for dd in range(2)]
                    smr_all = [K(("smr", h)) for h in range(4)]
                    P.add("act", OPC("activation", out=sm[:, 0:8], in_=psb[bB][:, 0:8], func=AF.Abs),
                          reads=[], writes=[pk[bB]] + smk_all)
                    P.add("dve", OPC("tensor_tensor", out=sm[:, 8:16], in0=sm[:, 0:8], in1=GATES[:, gi, 8:16], op=ALU.max),
                          reads=smk_all + [("G", gi)], writes=smr_all)
                    P.add("dve", OPC("reciprocal", out=sm[:, 8:16], in_=sm[:, 8:16]), reads=smr_all, writes=smr_all)
                    yield
                    for h in range(4):
                        banks = (bC, bD) if h % 2 == 0 else (bA, bB)
                        for dd in range(2):
                            bk = banks[dd]
                            cst = Cf if dd == 0 else cb
                            cstk = cfk if dd == 0 else cbk
                            P.add("pe", OPC("matmul", psb[bk][:, 0:256], lhsT=aT[:, dd, h, :], rhs=vp[:, dd, h, 0:256],
                                            start=True, stop=False),
                                  reads=[K("aT"), vpk], writes=[pk[bk]], group=("N", gi, h, dd))
                            for hf in range(2):
                                P.add("pe", OPC("matmul", psb[bk][:, 0:256], lhsT=qt[:, 2 * h + hf, :], rhs=cst[:, h, hf, 0:256],
                                                start=False, stop=(hf == 1)),
                                      reads=[qtk, cstk], writes=[pk[bk]], group=("N", gi, h, dd))
                        P.add("act", OPC("activation", out=hh[:, h * 256:(h + 1) * 256], in_=psb[banks[0]][:, 0:256], func=AF.Copy,
                                         scale=sm[:, 8 + h:9 + h]),
                              reads=[K(("smr", h))], writes=[pk[banks[0]], K(("hh", h))])
                        P.add("dve", OPC("scalar_tensor_tensor", out=hh[:, h * 256:(h + 1) * 256], in0=psb[banks[1]][:, 0:256],
                                         scalar=sm[:, 12 + h:13 + h], in1=hh[:, h * 256:(h + 1) * 256], op0=ALU.mult, op1=ALU.add),
                              reads=[K(("smr", h)), K(("hh", h))], writes=[pk[banks[1]], K(("hh", h))])
                        if h % 2 == 1:
                            yield
                if not last:
                    for h in range(4):
                        for hf in range(2):
                            bk = (bA, bB, bC, bD)[(2 * h + hf) % 4]
                            P.add("pe", OPC("matmul", psb[bk][:, 0:257], lhsT=ktok[:, h * 256 + hf * 128:h * 256 + (hf + 1) * 128],
                                            rhs=vp[:, d_, h, :], start=True, stop=True),
                                  reads=[ktokk, vpk], writes=[pk[bk]], group=("U", gi, d_, h, hf))
                            zk = K(("Z", h, hf))
                            if first:
                                P.add("act", OPC("activation", out=Z[:, h, hf, :], in_=psb[bk][:, 0:257], func=AF.Copy),
                                      reads=[], writes=[pk[bk], zk])
                            else:
                                P.add("dve", OPC("scalar_tensor_tensor", out=Z[:, h, hf, :], in0=Z[:, h, hf, :],
                                                 scalar=GATES[:, gp, 16 + d_ * 4 + h:17 + d_ * 4 + h], in1=psb[bk][:, 0:257],
                                                 op0=ALU.mult, op1=ALU.add),
                                      reads=[zk, ("G", gp)], writes=[pk[bk], zk])
                        yield
                w["prev"] = gi
                if mode == "full":
                    hks = [K(("hh", h)) for h in range(4)]
                    P.add("pool", OPC("tensor_tensor", out=hh[:], in0=hh[:], in1=so[:], op=ALU.mult),
                          reads=hks + [sok], writes=hks)
                    yield
                    for h in range(4):
                        P.add("act", OPC("activation", out=junk2[:], in_=hh[:, h * 256:(h + 1) * 256], func=AF.Square,
                                         accum_out=sm[:, h:h + 1]),
                              reads=[K(("hh", h))], writes=[K("junk2"), K(("sm", h, 0))])
                    s4 = [K(("sm", h, 0)) for h in range(4)]
                    s5 = [K(("sm", h, 1)) for h in range(4)]
                    P.add("act", OPC("activation", out=sm[:, 4:8], in_=sm[:, 0:4], func=AF.Sqrt, scale=1.0 / 256, bias=EPS),
                          reads=s4, writes=s5)
                    yield
                    P.add("dve", OPC("reciprocal", out=sm[:, 4:8], in_=sm[:, 4:8]), reads=s5, writes=s5)
                    P.add("dve", OPC("tensor_tensor", out=hm[:].rearrange("p (h v) -> p h v", h=4),
                                     in0=hh[:].rearrange("p (h v) -> p h v", h=4), in1=bc(sm, 4, [[1, 4], [0, 256]]), op=ALU.mult),
                          reads=hks + s5, writes=[K("hm")])
                    yield
                    pT = psb[bA][:, :].bitcast(BF16)
                    for kc in range(8):
                        P.add("pe", OPC("transpose", out=pT[:, kc * 128:(kc + 1) * 128], in_=hm[:, kc * 128:(kc + 1) * 128],
                                        identity=identb[:]),
                              reads=[K("hm"), "identb"], writes=[pk[bA]], group=("trh", gi))
                    yield
                    ht, htk = w["hmT"][0], K("hmT0")
                    P.add("dve", OPC("tensor_tensor", out=ht[:, 0:4, :], in0=pT[:, 0:512].rearrange("p (k t) -> p k t", k=4),
                                     in1=bc(cols, C_GMH, [[1, 4], [0, 128]]), op=ALU.mult),
                          reads=["cols"], writes=[pk[bA], htk])
                    for kc in range(4, 8):
                        P.add("act", OPC("activation", out=ht[:, kc, :], in_=pT[:, kc * 128:(kc + 1) * 128], func=AF.Copy,
                                         scale=cols[:, C_GMH + kc:C_GMH + kc + 1]),
                              reads=["cols"], writes=[pk[bA], htk], group=("hte", gi))
                    P.dma(HMT_d[b * 16 + c], ht[:], reads=[htk], writes=[("HMTd", b * 16 + c)], eng="pool")

            order_b = [17, 16] + list(range(15, -1, -1))
            order_f = [16, 17] + list(range(16))
            def drive(gens):
                gens = list(gens)
                while gens:
                    for g_ in list(gens):
                        try:
                            next(g_)
                        except StopIteration:
                            gens.remove(g_)

            for n_, c in enumerate(order_b):
                drive(chunk_step(b, c, 1, first=(n_ == 0), last=(n_ == len(order_b) - 1), mode="state")
                      for b in range(NB))
            for wi, name in enumerate(("wa", "wb", "wout")):
                for hb in range(2):
                    P.dma(wres[wi][:, :, hb * 512:(hb + 1) * 512], wall_d[WB[name] + hb], writes=[("wres", wi, hb)], eng="pool")
            for n_, c in enumerate(order_f):
                drive(chunk_step(b, c, 0, first=(n_ == 0), last=(n_ == len(order_f) - 1),
                                 mode=("full" if c < 16 else "state")) for b in range(NB))
            P.barrier(dummy[:])

        with phase() as ph:
            TT = 256
            hmTt = [sb("hmTt%d" % i, [128, 8, TT], BF16, ph) for i in range(2)]
            sga = [sb("sga%d" % i, [128, 8, TT], BF16, ph) for i in range(2)]
            sgb = [sb("sgb%d" % i, [128, 8, TT], BF16, ph) for i in range(2)]
            ut = [sb("ut%d" % i, [128, 8, TT], BF16, ph) for i in range(2)]
            vnt = [sb("vnt%d" % i, [128, 2, D], BF16, ph) for i in range(2)]
            xtb = [[sb("xt%d_%d" % (i, j), [128, D], F32, ph) for j in range(2)] for i in range(2)]
            y1 = [sb("y1_%d" % i, [128, 8, TT], F32, ph) for i in range(2)]
            gat = [sb("gat%d" % i, [128, 8, TT], BF16, ph) for i in range(2)]
            yT = [sb("yT%d" % i, [128, 8, TT], BF16, ph) for i in range(2)]
            tmp = [[sb("tmp%d_%d" % (i, j), [128, 512], F32, ph) for j in range(2)] for i in range(2)]
            xs2 = [sb("xs2%d" % i, [128, D], BF16, ph) for i in range(2)]
            xn2 = [sb("xn2%d" % i, [128, 8, TT], BF16, ph) for i in range(2)]
            junk3 = [sb("junk3%d" % i, [128, D], BF16, ph) for i in range(2)]
            tq = [[sb("tq%d_%d" % (i, j), [128, TT], F32, ph) for j in range(4)] for i in range(2)]
            st3 = sb("st3", [128, 16], F32, ph)
            ones1 = consts[0:1, 3, :]

            def mix_tile(T2):
                b = T2 // 8
                t = (T2 % 8) // 2
                hf2 = T2 % 2
                T0 = b * 4 + t
                p2 = T2 % 2
                itl = [0]

                def nbank():
                    bk = 4 * p2 + itl[0] % 4
                    itl[0] += 1
                    return bk
                hm_, sga_, sgb_, ut_, vn_ = hmTt[p2], sga[p2], sgb[p2], ut[p2], vnt[p2]
                y1_, gat_, yT_, xn_ = y1[p2], gat[p2], yT[p2], xn2[p2]
                kk = lambda n_: (n_, p2)
                for j in range(2):
                    ch = T2 * 2 + j
                    P.dma(hm_[:, :, j * 128:(j + 1) * 128], HMT_d[ch], reads=[("HMTd", ch)], writes=[kk(("hmTt", j))])
                    P.dma(vn_[:, j, :], VN_d[ch], reads=[("VNd", ch)], writes=[kk(("vnt", j))])
                P.dma(sga_[:], SGA_d[T0][:, :, hf2 * TT:(hf2 + 1) * TT], reads=[("SGAd", T0)], writes=[kk("sga")])
                P.dma(ut_[:], UT_d[T0][:, :, hf2 * TT:(hf2 + 1) * TT], reads=[("UTd", T0)], writes=[kk("ut")])
                P.dma(sgb_[:], SGB_d[T0][:, :, hf2 * TT:(hf2 + 1) * TT], reads=[("SGBd", T0)], writes=[kk("sgb")])
                for j in range(2):
                    ch = T2 * 2 + j
                    cl = ch % 16
                    P.dma(xtb[p2][j][:], x_d[b, cl * 128:(cl + 1) * 128, :], writes=[kk(("xt", j))])
                yield
                hks_ = [kk(("hmTt", j)) for j in range(2)]
                for cc in range(8):
                    bank = nbank()
                    for kc in range(8):
                        P.add("pe", OPC("matmul", psb[bank][:, 0:TT], lhsT=wres[0][:, kc, cc * 128:(cc + 1) * 128], rhs=hm_[:, kc, :],
                                        start=(kc == 0), stop=(kc == 7)),
                              reads=[("wres", 0, cc // 4)] + hks_, writes=[pk[bank]], group=("ya", T2, cc))
                    P.add("dve", OPC("tensor_tensor", out=y1_[:, cc, :], in0=psb[bank][:, 0:TT], in1=sga_[:, cc, :], op=ALU.mult),
                          reads=[kk("sga")], writes=[pk[bank], kk(("y1", cc))])
                    if cc % 4 == 3:
                        yield
                for g0 in range(0, 8, 4):
                    bks_ = []
                    for g in range(g0, g0 + 4):
                        bank = nbank()
                        bks_.append(bank)
                        for j in range(2):
                            P.add("pe", OPC("matmul", psb[bank][:, j * 128:(j + 1) * 128], lhsT=vn_[:, j, g * 128:(g + 1) * 128],
                                            rhs=wsT[:, g, :], start=True, stop=True),
                                  reads=[kk(("vnt", j)), "wsT"], writes=[pk[bank]], group=("sgu", T2, g))
                    for g in range(g0, g0 + 4):
                        bank = bks_[g - g0]
                        P.add("dve", OPC("tensor_tensor", out=tq[p2][g % 4][:].rearrange("p (j q) -> p j q", j=2),
                                         in0=bc(psb[bank], 0, [[128, 2], [1, 128]]),
                                         in1=bc(rowsb, R_BS + g * 128, [[0, 2], [1, 128]]), op=ALU.add),
                              reads=["rowsb"], writes=[pk[bank], kk(("tq", g % 4))])
                    for g in range(g0, g0 + 4):
                        P.add(("dve", "pool")[g % 2], OPC("tensor_tensor", out=gat_[:, g, :], in0=tq[p2][g % 4][:], in1=ut_[:, g, :], op=ALU.mult),
                              reads=[kk(("tq", g % 4)), kk("ut")], writes=[kk(("gat", g))])
                    yield
                gks = [kk(("gat", g)) for g in range(8)]
                for cc in range(8):
                    bank = nbank()
                    for kc in range(8):
                        P.add("pe", OPC("matmul", psb[bank][:, 0:TT], lhsT=wres[1][:, kc, cc * 128:(cc + 1) * 128], rhs=gat_[:, kc, :],
                                        start=(kc == 0), stop=(kc == 7)),
                              reads=[("wres", 1, cc // 4)] + gks, writes=[pk[bank]], group=("yb", T2, cc))
                    tm = tmp[p2][cc % 2]
                    tk = kk("tmp%d" % (cc % 2))
                    P.add("dve", OPC("tensor_tensor", out=tm[:, 0:TT], in0=psb[bank][:, 0:TT], in1=sgb_[:, cc, :], op=ALU.mult),
                          reads=[kk("sgb")], writes=[pk[bank], tk])
                    P.add("pool", OPC("tensor_tensor", out=yT_[:, cc, :], in0=tm[:, 0:TT], in1=y1_[:, cc, :], op=ALU.add),
                          reads=[tk, kk(("y1", cc))], writes=[kk(("yT", cc))])
                    if cc % 4 == 3:
                        yield
                yks = [kk(("yT", cc)) for cc in range(8)]
                for j in range(2):
                    ch = T2 * 2 + j
                    xt_ = xtb[p2][j]
                    xtk = kk(("xt", j))
                    for nb_ in range(2):
                        bank = nbank()
                        for kc in range(8):
                            P.add("pe", OPC("matmul", psb[bank][:, :], lhsT=yT_[:, kc, j * 128:(j + 1) * 128],
                                            rhs=wres[2][:, kc, nb_ * 512:(nb_ + 1) * 512], start=(kc == 0), stop=(kc == 7)),
                                  reads=[("wres", 2, nb_)] + yks, writes=[pk[bank]], group=("mix", T2, j, nb_))
                        tm = tmp[p2][nb_]
                        tk = kk("tmp%d" % nb_)
                        P.add("dve", OPC("tensor_tensor", out=tm[:], in0=psb[bank][:, :], in1=g1bc[:, b, nb_ * 512:(nb_ + 1) * 512], op=ALU.mult),
                              reads=["gbc"], writes=[pk[bank], tk])
                        P.add("pool", OPC("tensor_tensor", out=xt_[:, nb_ * 512:(nb_ + 1) * 512], in0=tm[:], in1=xt_[:, nb_ * 512:(nb_ + 1) * 512],
                                          op=ALU.add), reads=[tk, xtk], writes=[xtk])
                    yield
                    P.dma(X1_d[ch], xt_[:], reads=[xtk], writes=[("X1d", ch)], eng="pool")
                    o4 = 8 * p2 + 4 * j
                    sk_ = "st3_%d" % (o4)
                    P.add("act", OPC("activation", out=junk3[p2][:], in_=xt_[:], func=AF.Square, accum_out=st3[:, o4:o4 + 1]),
                          reads=[xtk], writes=[kk("junk3"), sk_])
                    P.add("act", OPC("activation", out=st3[:, o4 + 1:o4 + 2], in_=st3[:, o4:o4 + 1], func=AF.Sqrt,
                                     scale=1.0 / D, bias=EPS), reads=[sk_], writes=[sk_ + "b"])
                    yield
                    P.add("dve", OPC("reciprocal", out=st3[:, o4 + 2:o4 + 3], in_=st3[:, o4 + 1:o4 + 2]), reads=[sk_ + "b"], writes=[sk_ + "c"])
                    xs = xs2[p2]
                    xsk = kk("xs2")
                    P.add("dve", OPC("tensor_scalar", out=xs[:], in0=xt_[:], scalar1=st3[:, o4 + 2:o4 + 3], scalar2=None, op0=ALU.mult),
                          reads=[xtk, sk_ + "c"], writes=[xsk])
                    yield
                    bank = nbank()
                    pT = psb[bank][:, :].bitcast(BF16)
                    for kc in range(8):
                        P.add("pe", OPC("transpose", out=pT[:, kc * 128:(kc + 1) * 128], in_=xs[:, kc * 128:(kc + 1) * 128], identity=identb[:]),
                              reads=[xsk, "identb"], writes=[pk[bank]], group=("tr2", ch))
                    yield
                    for kc in range(8):
                        P.add("act", OPC("activation", out=xn_[:, kc, j * 128:(j + 1) * 128], in_=pT[:, kc * 128:(kc + 1) * 128],
                                         func=AF.Identity, scale=gcol2[:, kc, b:b + 1], bias=modc[:, 24 + kc, b:b + 1]),
                              reads=["gcol"], writes=[pk[bank], kk("xn2")], group=("ev2", ch))
                P.dma(XN2_d[T2], xn_[:], reads=[kk("xn2")], writes=[("XN2d", T2)], eng="pool")

            def drive2(gens):
                gens = list(gens)
                while gens:
                    for g_ in list(gens):
                        try:
                            next(g_)
                        except StopIteration:
                            gens.remove(g_)

            for T2 in range(0, NB * 8, 2):
                drive2([mix_tile(T2), mix_tile(T2 + 1)])
            P.barrier(dummy[:])

        ph_outer.__exit__()
        ph_outer2 = phase()
        ph_outer2.__enter__()
        w2b = sb("w2b", [128, 32, 1024], BF16)
        with phase() as ph:
            w1b = sb("w1b", [128, 8, 4096], BF16, ph)
            for i in range(8):
                P.dma(w1b[:, :, i * 512:(i + 1) * 512], wall_d[WB["w1"] + i], writes=[("w1b", i)], eng="pool")
            for i in range(8):
                fg, nb_ = i // 2, i % 2
                P.dma(w2b[:, fg * 8:(fg + 1) * 8, nb_ * 512:(nb_ + 1) * 512], wall_d[WB["w2"] + i], writes=[("w2b", i)], eng="pool")
            xn2t = [sb("xn2t%d" % i, [128, 8, 256], BF16, ph) for i in range(2)]
            h1t = [sb("h1t%d" % i, [128, 32, 256], BF16, ph) for i in range(2)]
            sq = [sb("sq%d" % i, [128, 256], BF16, ph) for i in range(2)]
            it = 0
            for T2 in range(NB * 8):
                xn_, xnk = xn2t[T2 % 2], "xn2t%d" % (T2 % 2)
                h_, hk = h1t[T2 % 2], "h1t%d" % (T2 % 2)
                P.dma(xn_[:], XN2_d[T2], reads=[("XN2d", T2)], writes=[xnk])
                for fc in range(32):
                    bank = it % 4
                    s_ = sq[it % 2]
                    sk = "sq%d" % (it % 2)
                    it += 1
                    for kc in range(8):
                        P.add("pe", OPC("matmul", psb[bank][:, 0:256], lhsT=w1b[:, kc, fc * 128:(fc + 1) * 128], rhs=xn_[:, kc, :],
                            start=(kc == 0), stop=(kc == 7)),
                            reads=[("w1b", fc // 4), xnk], writes=[pk[bank]], group=("h1", it))
                    P.add("act", OPC("activation", out=s_[:], in_=psb[bank][:, 0:256], func=AF.Square),
                          reads=[], writes=[pk[bank], sk])
                    P.add("dve", OPC("scalar_tensor_tensor", out=h_[:, fc, :], in0=psb[bank][:, 0:256], scalar=0.0, in1=s_[:], op0=ALU.is_gt, op1=ALU.mult),
                        reads=[sk], writes=[pk[bank], hk])
                P.dma(H1T_d[T2], h_[:], reads=[hk], writes=[("H1Td", T2)], eng="pool")
            P.barrier(dummy[:])

        with phase() as ph:
            h1t = [sb("h1t%d" % i, [128, 32, 256], BF16, ph) for i in range(2)]
            x1t = [sb("x1t%d" % i, [128, 2, D], F32, ph) for i in range(2)]
            tmp = [sb("tmp%d" % i, [128, 512], F32, ph) for i in range(2)]
            ot = [sb("ot%d" % i, [128, D], F32, ph) for i in range(2)]
            junk4 = sb("junk4", [128, D], F32, ph)
            st4 = sb("st4", [128, 4], F32, ph)
            it = 0
            for T2 in range(NB * 8):
                b = T2 // 8
                h_, hk = h1t[T2 % 2], "h1t%d" % (T2 % 2)
                x1_, x1k = x1t[T2 % 2], "x1t%d" % (T2 % 2)
                P.dma(h_[:], H1T_d[T2], reads=[("H1Td", T2)], writes=[hk])
                for j in range(2):
                    ch = T2 * 2 + j
                    P.dma(x1_[:, j, :], X1_d[ch], reads=[("X1d", ch)], writes=[(x1k, j)])
                for j in range(2):
                    ch = T2 * 2 + j
                    o_, ok = ot[ch % 2], "ot%d" % (ch % 2)
                    for nb_ in range(2):
                        bank = it % 4
                        it += 1
                        for fc in range(32):
                            P.add("pe", OPC("matmul", psb[bank][:, :], lhsT=h_[:, fc, j * 128:(j + 1) * 128],
                                rhs=w2b[:, fc, nb_ * 512:(nb_ + 1) * 512], start=(fc == 0), stop=(fc == 31)),
                                reads=[("w2b", i_) for i_ in range(8)] + [hk], writes=[pk[bank]], group=("o2", it))
                        tm = tmp[nb_]
                        tk = "tmp%d" % nb_
                        P.add("dve", OPC("tensor_tensor", out=tm[:], in0=psb[bank][:, :], in1=g2bc[:, b, nb_ * 512:(nb_ + 1) * 512], op=ALU.mult),
                            reads=["gbc"], writes=[pk[bank], tk])
                        P.add("pool", OPC("tensor_tensor", out=x1_[:, j, nb_ * 512:(nb_ + 1) * 512], in0=tm[:], in1=x1_[:, j, nb_ * 512:(nb_ + 1) * 512],
                            op=ALU.add), reads=[tk, (x1k, j)], writes=[(x1k, j)])
                    P.add("act", OPC("activation", out=junk4[:], in_=x1_[:, j, :], func=AF.Square,
                                                                       accum_out=st4[:, 0:1]),
                          reads=[(x1k, j)], writes=["junk4", "st4"])
                    P.add("act", OPC("activation", out=st4[:, 1:2], in_=st4[:, 0:1], func=AF.Sqrt,
                                                        scale=1.0 / D, bias=EPS), reads=["st4"], writes=["st4b"])
                    P.add("dve", OPC("reciprocal", out=st4[:, 2:3], in_=st4[:, 1:2]), reads=["st4b"], writes=["st4c"])
                    P.add("dve", OPC("scalar_tensor_tensor", out=o_[:], in0=x1_[:, j, :], scalar=st4[:, 2:3], in1=rowsb[:, R_NF:R_NF + D],
                        op0=ALU.mult, op1=ALU.mult), reads=[(x1k, j), "st4c", "rowsb"], writes=[ok])
                    tok0 = (ch % 16) * 128
                    P.dma(out_d[b, tok0:tok0 + 128, :], o_[:], reads=[ok], writes=[("outd", ch)], eng="pool")
        P.emit()
    return nc


def _host_inputs(x, c, ctx, c_ctx, norm1, norm2, w_mod, b_mod, w_in, conv_qk, b_gate, g_mh,
                 w_a, w_s, b_s, g_sgu, w_b, w_out, w1, w2, norm_f):
    f = np.float32

    def blocks(W):
        K, N = W.shape
        out = []
        for kg in range(K // 1024):
            for nb_ in range(N // 512):
                blk = W[kg * 1024:(kg + 1) * 1024, nb_ * 512:(nb_ + 1) * 512]
                out.append(blk.reshape(8, 128, 512).transpose(1, 0, 2))
        return out

    w_in0 = np.asarray(w_in[0], f)
    wl = []
    for n in ("q", "k", "v", "o", "u", "vg", "ga", "gb"):
        c0 = WIN_COLS[n]
        wl += blocks(w_in0[:, c0:c0 + 1024])
    wl += blocks(np.asarray(w_a[0], f)) + blocks(np.asarray(w_b[0], f)) + blocks(np.asarray(w_out[0], f))
    wl += blocks(np.asarray(w1[0], f))
    wl += blocks(np.asarray(w2[0], f))
    wall = np.ascontiguousarray(np.stack(wl, 0))
    assert wall.shape[0] == NWB
    wg = np.ascontiguousarray(w_in0[:, 3072:3088].reshape(8, 128, 16).transpose(1, 0, 2))
    wmod = np.ascontiguousarray(np.stack(blocks(np.asarray(w_mod[0], f)), 0))
    bm = np.asarray(b_mod[0], f)
    bmodc = np.ascontiguousarray(bm.reshape(48, 128).T)
    bmodr = np.ascontiguousarray(bm.reshape(1, -1))
    colsv = np.zeros((128, NCOLS), f)
    colsv[:, C_N1:C_N1 + 8] = np.asarray(norm1[0], f).reshape(8, 128).T
    colsv[:, C_N2:C_N2 + 8] = np.asarray(norm2[0], f).reshape(8, 128).T
    colsv[:, C_GMH:C_GMH + 8] = np.asarray(g_mh[0], f).reshape(8, 128).T
    cq = np.asarray(conv_qk[0], f)
    colsv[:, C_CONV:C_CONV + 48] = cq.reshape(3, 16, 128).transpose(2, 1, 0).reshape(128, 48)
    rowsv = np.concatenate([np.asarray(b_gate[0], f).reshape(-1), np.asarray(g_sgu[0], f).reshape(-1),
                            np.asarray(norm_f, f).reshape(-1), np.asarray(b_s[0], f).reshape(-1)]).reshape(1, -1)
    wsT = np.ascontiguousarray(np.asarray(w_s[0], f).transpose(2, 0, 1))
    eye = np.eye(128, dtype=f)
    triU = np.triu(np.ones((128, 128), f))
    triL = np.tril(np.ones((128, 128), f))
    consts = np.ascontiguousarray(np.stack([eye, triU, triL, np.ones((128, 128), f)], 1))
    maps = []
    xx = np.asarray(x, f)
    cc = np.asarray(c, f)
    cx = np.asarray(ctx, f)
    ccx = np.asarray(c_ctx, f)
    for i in range(8):
        sc = np.stack([cc[2 * i], cc[2 * i + 1], ccx], 1)
        scT = np.ascontiguousarray(sc.reshape(8, 128, 3).transpose(1, 0, 2))
        maps.append(dict(x=np.ascontiguousarray(xx[2 * i:2 * i + 2]), ctx=np.ascontiguousarray(cx[2 * i:2 * i + 2]),
                         scT=scT, wmod=wmod, bmodc=bmodc, bmodr=bmodr, wall=wall, wg=wg, cols=colsv,
                         rows=np.ascontiguousarray(rowsv), wsT=wsT, consts=consts))
    return maps


def kernel(**inputs):
    maps = _host_inputs(**inputs)
    nc = build_nc()
    res = run_bass_kernel_spmd(nc, maps, core_ids=list(range(8)))
    out = np.concatenate([np.asarray(r["out"]) for r in res.results], axis=0)
    return out.astype(np.float32)
```
